# Optimizing a Trainium2 kernel written in Bass

```python
import jax, jax.numpy as jnp
from jax import lax
import numpy as np

D_MODEL = 1024
BATCH = 2
SEQ = 16384
DEPTH = 1
DEC_BATCH = 16
DEC_SEQ = 64
PAST_LEN = 4096

CHUNK = 64
D_PLE = 256
D_FF = 2816
CONV_CH = 1024
CONV_WIDTH = 31
SSD_HEADS = 16
SSD_HEAD_DIM = 64
SSD_INNER = SSD_HEADS * SSD_HEAD_DIM
SSD_GROUPS = 2
SSD_STATE = 128
SSD_CONV_WIDTH = 4
SSD_CONV_DIM = SSD_INNER + 2 * SSD_GROUPS * SSD_STATE
MIX_WIDTH = CONV_CH + SSD_INNER
IN_PROJ = 2 * CONV_CH + SSD_INNER + SSD_CONV_DIM + SSD_HEADS
EPS = 1e-6

kernel_name = "hybrid_conformer_ssd_streaming_step"


def rms_norm(x, g):
    xf = x.astype(jnp.float32)
    y = xf * lax.rsqrt(jnp.mean(xf * xf, axis=-1, keepdims=True) + EPS)
    return (y * g.astype(jnp.float32)).astype(x.dtype)


def layer_norm(x, g, b):
    xf = x.astype(jnp.float32)
    mu = jnp.mean(xf, axis=-1, keepdims=True)
    xc = xf - mu
    y = xc * lax.rsqrt(jnp.mean(xc * xc, axis=-1, keepdims=True) + EPS)
    return (y * g.astype(jnp.float32) + b.astype(jnp.float32)).astype(x.dtype)


def swiglu_ffn(x, w_gate, w_up, w_down):
    return (jax.nn.silu(x @ w_gate) * (x @ w_up)) @ w_down


def causal_dwconv(x_pad, w, b):
    c = x_pad.shape[-1]
    y = lax.conv_general_dilated(x_pad, w[:, None, :].astype(x_pad.dtype), (1,), 'VALID',
                                 dimension_numbers=('NWC', 'WIO', 'NWC'),
                                 feature_group_count=c)
    return y + b.astype(x_pad.dtype)


def ssd_chunk_step(S, inp, A):
    x, dt, B, C = inp
    L = x.shape[1]
    acum = jnp.cumsum(dt * A, axis=1)
    causal = jnp.tril(jnp.ones((L, L), dtype=bool))[None, :, :, None, None]
    diff = acum[:, :, None] - acum[:, None, :]
    decay = jnp.where(causal, jnp.exp(jnp.where(causal, diff, 0.0)), 0.0)
    cb = jnp.einsum('btgn,bsgn->btsg', C, B)
    w = cb[..., None] * decay * dt[:, None]
    y = jnp.einsum('btsgj,bsgjp->btgjp', w, x)
    y = y + jnp.einsum('btgn,bgjpn->btgjp', C, S) * jnp.exp(acum)[..., None]
    w_end = jnp.exp(acum[:, -1:] - acum) * dt
    S_new = (S * jnp.exp(acum[:, -1])[..., None, None]
             + jnp.einsum('bsgj,bsgn,bsgjp->bgjpn', w_end, B, x))
    return S_new, y


def ssd_scan(x, dt, A, B, C, S0):
    b, T = x.shape[0], x.shape[1]
    L = min(CHUNK, T)
    nc = T // L

    def to_blocks(a):
        return jnp.moveaxis(a.reshape((b, nc, L) + a.shape[2:]), 1, 0)

    S, ys = lax.scan(lambda s, inp: ssd_chunk_step(s, inp, A), S0,
                     (to_blocks(x), to_blocks(dt), to_blocks(B), to_blocks(C)))
    y = jnp.moveaxis(ys, 0, 1).reshape(x.shape)
    return y, S


def hybrid_mixer(u, conv_buf, xbc_buf, ssm, conv_dw_w, conv_dw_b, conv_ln_g, conv_ln_b,
                 ssd_conv_w, ssd_conv_b, ssd_dt_bias, ssd_A_log, ssd_D, ssd_norm):
    b, T, _ = u.shape
    f32 = jnp.float32
    G, J, P, N = SSD_GROUPS, SSD_HEADS // SSD_GROUPS, SSD_HEAD_DIM, SSD_STATE
    o1 = CONV_CH
    o2 = 2 * CONV_CH
    o3 = o2 + SSD_INNER
    o4 = o3 + SSD_CONV_DIM
    c_val, c_gate, z, xbc, dt_raw = u[..., :o1], u[..., o1:o2], u[..., o2:o3], u[..., o3:o4], u[..., o4:]

    a = c_val * jax.nn.sigmoid(c_gate)
    a_pad = jnp.concatenate([conv_buf.astype(a.dtype), a], axis=1)
    new_conv_buf = a_pad[:, -(CONV_WIDTH - 1):]
    c = jax.nn.silu(layer_norm(causal_dwconv(a_pad, conv_dw_w, conv_dw_b), conv_ln_g, conv_ln_b))

    xbc_pad = jnp.concatenate([xbc_buf.astype(xbc.dtype), xbc], axis=1)
    new_xbc_buf = xbc_pad[:, -(SSD_CONV_WIDTH - 1):]
    xbc = jax.nn.silu(causal_dwconv(xbc_pad, ssd_conv_w, ssd_conv_b))
    xs = xbc[..., :SSD_INNER]
    Bm = xbc[..., SSD_INNER:SSD_INNER + G * N].astype(f32).reshape(b, T, G, N)
    Cm = xbc[..., SSD_INNER + G * N:].astype(f32).reshape(b, T, G, N)
    dt = jax.nn.softplus(dt_raw.astype(f32) + ssd_dt_bias.astype(f32)).reshape(b, T, G, J)
    A = -jnp.exp(ssd_A_log.astype(f32)).reshape(G, J)
    xh = xs.astype(f32).reshape(b, T, G, J, P)
    S0 = ssm.astype(f32).reshape(b, G, J, P, N)
    y, S = ssd_scan(xh, dt, A, Bm, Cm, S0)
    y = y + ssd_D.astype(f32).reshape(G, J)[:, :, None] * xh
    y = y.reshape(b, T, SSD_INNER).astype(u.dtype)
    y = rms_norm(y * jax.nn.silu(z), ssd_norm)

    m = jnp.concatenate([c, y], axis=-1)
    return m, new_conv_buf, new_xbc_buf, S.reshape(b, SSD_HEADS, P, N).astype(u.dtype)


def trunk_layer(x, p, conv_buf, xbc_buf, ssm, w):
    (ffn1_norm, ffn1_w_gate, ffn1_w_up, ffn1_w_down, mix_norm, w_in,
     conv_dw_w, conv_dw_b, conv_ln_g, conv_ln_b, ssd_conv_w, ssd_conv_b,
     ssd_dt_bias, ssd_A_log, ssd_D, ssd_norm, w_out,
     ffn2_norm, ffn2_w_gate, ffn2_w_up, ffn2_w_down,
     ple_norm, ple_w_proj, ple_w_gate) = w
    h = x + 0.5 * swiglu_ffn(rms_norm(x, ffn1_norm), ffn1_w_gate, ffn1_w_up, ffn1_w_down)
    u = rms_norm(h, mix_norm) @ w_in
    m, nc, nx, ns = hybrid_mixer(u, conv_buf, xbc_buf, ssm, conv_dw_w, conv_dw_b, conv_ln_g, conv_ln_b,
                                 ssd_conv_w, ssd_conv_b, ssd_dt_bias, ssd_A_log, ssd_D, ssd_norm)
    h = h + m @ w_out
    h = h + 0.5 * swiglu_ffn(rms_norm(h, ffn2_norm), ffn2_w_gate, ffn2_w_up, ffn2_w_down)
    gate = jax.nn.sigmoid(rms_norm(h, ple_norm) @ ple_w_gate)
    h = h + gate * (p.astype(h.dtype) @ ple_w_proj)
    return h, nc, nx, ns


def setup_inputs(seed: int = 0) -> dict:
    key = jax.random.key(seed)
    ks = jax.random.split(key, 40)
    f32 = jnp.float32

    def nrm(k, shape, scale):
        return jax.random.normal(k, shape, f32) * scale

    def gain(k, shape):
        return 1.0 + 0.05 * jax.random.normal(k, shape, f32)

    dt0 = jnp.exp(jax.random.uniform(ks[30], (DEPTH, SSD_HEADS), f32)
                  * (np.log(0.1) - np.log(0.001)) + np.log(0.001))
    dt_bias = dt0 + jnp.log(-jnp.expm1(-dt0))
    A_log = jnp.log(jax.random.uniform(ks[31], (DEPTH, SSD_HEADS), f32, 1.0, 16.0))

    return {
        "x_prompt": nrm(ks[0], (BATCH, SEQ, D_MODEL), 1.0),
        "x_sample": nrm(ks[1], (DEC_BATCH, DEC_SEQ, D_MODEL), 1.0),
        "p_prompt": nrm(ks[2], (DEPTH, BATCH, SEQ, D_PLE), 1.0),
        "p_sample": nrm(ks[3], (DEPTH, DEC_BATCH, DEC_SEQ, D_PLE), 1.0),
        "state_conv": nrm(ks[4], (DEPTH, DEC_BATCH, CONV_WIDTH - 1, CONV_CH), 0.5),
        "state_ssd_conv": nrm(ks[5], (DEPTH, DEC_BATCH, SSD_CONV_WIDTH - 1, SSD_CONV_DIM), 0.5),
        "state_ssd": nrm(ks[6], (DEPTH, DEC_BATCH, SSD_HEADS, SSD_HEAD_DIM, SSD_STATE), 0.5),
        "ffn1_norm": gain(ks[7], (DEPTH, D_MODEL)),
        "ffn1_w_gate": nrm(ks[8], (DEPTH, D_MODEL, D_FF), D_MODEL ** -0.5),
        "ffn1_w_up": nrm(ks[9], (DEPTH, D_MODEL, D_FF), D_MODEL ** -0.5),
        "ffn1_w_down": nrm(ks[10], (DEPTH, D_FF, D_MODEL), D_FF ** -0.5),
        "mix_norm": gain(ks[11], (DEPTH, D_MODEL)),
        "w_in": nrm(ks[12], (DEPTH, D_MODEL, IN_PROJ), D_MODEL ** -0.5),
        "conv_dw_w": nrm(ks[13], (DEPTH, CONV_WIDTH, CONV_CH), CONV_WIDTH ** -0.5),
        "conv_dw_b": nrm(ks[14], (DEPTH, CONV_CH), 0.01),
        "conv_ln_g": gain(ks[15], (DEPTH, CONV_CH)),
        "conv_ln_b": nrm(ks[16], (DEPTH, CONV_CH), 0.01),
        "ssd_conv_w": nrm(ks[17], (DEPTH, SSD_CONV_WIDTH, SSD_CONV_DIM), SSD_CONV_WIDTH ** -0.5),
        "ssd_conv_b": nrm(ks[18], (DEPTH, SSD_CONV_DIM), 0.01),
        "ssd_dt_bias": dt_bias,
        "ssd_A_log": A_log,
        "ssd_D": gain(ks[19], (DEPTH, SSD_HEADS)),
        "ssd_norm": gain(ks[20], (DEPTH, SSD_INNER)),
        "w_out": nrm(ks[21], (DEPTH, MIX_WIDTH, D_MODEL), MIX_WIDTH ** -0.5),
        "ffn2_norm": gain(ks[22], (DEPTH, D_MODEL)),
        "ffn2_w_gate": nrm(ks[23], (DEPTH, D_MODEL, D_FF), D_MODEL ** -0.5),
        "ffn2_w_up": nrm(ks[24], (DEPTH, D_MODEL, D_FF), D_MODEL ** -0.5),
        "ffn2_w_down": nrm(ks[25], (DEPTH, D_FF, D_MODEL), D_FF ** -0.5),
        "ple_norm": gain(ks[26], (DEPTH, D_MODEL)),
        "ple_w_proj": nrm(ks[27], (DEPTH, D_PLE, D_MODEL), D_PLE ** -0.5),
        "ple_w_gate": nrm(ks[28], (DEPTH, D_MODEL, D_MODEL), D_MODEL ** -0.5),
        "final_norm": gain(ks[29], (D_MODEL,)),
    }


def reference(x_prompt, x_sample, p_prompt, p_sample, state_conv, state_ssd_conv, state_ssd,
              ffn1_norm, ffn1_w_gate, ffn1_w_up, ffn1_w_down, mix_norm, w_in,
              conv_dw_w, conv_dw_b, conv_ln_g, conv_ln_b, ssd_conv_w, ssd_conv_b,
              ssd_dt_bias, ssd_A_log, ssd_D, ssd_norm, w_out,
              ffn2_norm, ffn2_w_gate, ffn2_w_up, ffn2_w_down,
              ple_norm, ple_w_proj, ple_w_gate, final_norm):
    bp = x_prompt.shape[0]
    h_p, h_s = x_prompt, x_sample
    conv_p, xbc_p, ssm_p, conv_s, xbc_s, ssm_s = [], [], [], [], [], []
    for i in range(DEPTH):
        w_i = (ffn1_norm[i], ffn1_w_gate[i], ffn1_w_up[i], ffn1_w_down[i], mix_norm[i], w_in[i],
               conv_dw_w[i], conv_dw_b[i], conv_ln_g[i], conv_ln_b[i], ssd_conv_w[i], ssd_conv_b[i],
               ssd_dt_bias[i], ssd_A_log[i], ssd_D[i], ssd_norm[i], w_out[i],
               ffn2_norm[i], ffn2_w_gate[i], ffn2_w_up[i], ffn2_w_down[i],
               ple_norm[i], ple_w_proj[i], ple_w_gate[i])
        zc = jnp.zeros((bp, CONV_WIDTH - 1, CONV_CH), x_prompt.dtype)
        zx = jnp.zeros((bp, SSD_CONV_WIDTH - 1, SSD_CONV_DIM), x_prompt.dtype)
        zs = jnp.zeros((bp, SSD_HEADS, SSD_HEAD_DIM, SSD_STATE), x_prompt.dtype)
        h_p, c1, x1, s1 = trunk_layer(h_p, p_prompt[i], zc, zx, zs, w_i)
        h_s, c2, x2, s2 = trunk_layer(h_s, p_sample[i], state_conv[i], state_ssd_conv[i], state_ssd[i], w_i)
        conv_p.append(c1); xbc_p.append(x1); ssm_p.append(s1)
        conv_s.append(c2); xbc_s.append(x2); ssm_s.append(s2)
    y_prompt = rms_norm(h_p, final_norm)
    y_sample = rms_norm(h_s, final_norm)
    return (y_prompt, y_sample,
            jnp.stack(conv_p), jnp.stack(xbc_p), jnp.stack(ssm_p),
            jnp.stack(conv_s), jnp.stack(xbc_s), jnp.stack(ssm_s))
```

```python
import contextlib
import numpy as np
import concourse.bass as bass
import concourse.mybir as mybir
from concourse.bass_utils import run_bass_kernel_spmd

F32 = mybir.dt.float32
BF16 = mybir.dt.bfloat16
AF = mybir.ActivationFunctionType
ALU = mybir.AluOpType

D = 1024
DFF = 2816
NFC = DFF // 128
INP = 4624
TT = 512
import os as _os
PEND_MAX = int(_os.environ.get('PEND_MAX', '2'))
STATS_DELAY = int(_os.environ.get('STATS_DELAY', '1'))
EPS = 1e-6


class Buf:
    __slots__ = ("name", "last_w", "readers")

    def __init__(self, name="", fence=()):
        self.name = name
        self.last_w = None
        self.readers = list(fence)


class Op:
    __slots__ = ("eng", "idx", "fn", "deps", "is_target", "semval", "dma_sem", "dma_val", "dma_prev")

    def __init__(self, eng, idx, fn, deps):
        self.eng, self.idx, self.fn, self.deps = eng, idx, fn, deps
        self.is_target = False
        self.semval = None
        self.dma_sem = None
        self.dma_val = None
        self.dma_prev = None


class EngQ:
    def __init__(self, name):
        self.name = name
        self.ops = []
        self.sem = None


class Sched:
    ENG_NAMES = ("pe", "act", "dve", "pool", "sp")

    def __init__(self, nc, dma_ring=24):
        self.nc = nc
        self.q = {n: EngQ(n) for n in self.ENG_NAMES}
        self.dma_ring = dma_ring
        self.dma_sems = []
        self.dma_count = 0
        self.dma_cnt_q = {}
        self.dma_ops = []

    def op(self, eng, fn, reads=(), writes=(), extra=(), dma=False):
        q = self.q[eng]
        deps = []
        for b in reads:
            if b.last_w is not None:
                deps.append(b.last_w)
        for b in writes:
            if b.last_w is not None:
                deps.append(b.last_w)
            deps.extend(b.readers)
        deps.extend(extra)
        o = Op(q, len(q.ops), fn, deps)
        q.ops.append(o)
        for b in writes:
            b.last_w = o
            b.readers = []
        for b in reads:
            if b.last_w is not o:
                b.readers.append(o)
        return o

    def dma(self, eng, fn, reads=(), writes=(), extra=()):
        o = self.op(eng, fn, reads, writes, extra, dma=True)
        half = self.dma_ring // 2
        cnt = self.dma_cnt_q.setdefault(eng, 0)
        self.dma_cnt_q[eng] = cnt + 1
        o.dma_sem = (cnt % half) + (half if eng == "pool" else 0)
        self.dma_count += 1
        self.dma_ops.append(o)
        return o

    def finalize(self):
        for q in self.q.values():
            for o in q.ops:
                for d in o.deps:
                    d.is_target = True
        last = [None] * self.dma_ring
        cnt = [0] * self.dma_ring
        for o in self.dma_ops:
            s = o.dma_sem
            o.dma_prev = last[s]
            cnt[s] += 16
            o.dma_val = cnt[s]
            last[s] = o
        for q in self.q.values():
            v = 0
            for o in q.ops:
                if o.dma_sem is None and o.is_target:
                    v += 1
                    o.semval = v

    def emit(self, final_waits=()):
        nc = self.nc
        self.finalize()
        with contextlib.ExitStack() as es:
            for q in self.q.values():
                q.sem = es.enter_context(nc.semaphore("s_" + q.name))
            self.dma_sems = [es.enter_context(nc.semaphore(f"d{i}")) for i in range(self.dma_ring)]
            block = es.enter_context(nc.Block())
            engmap = {"pe": block.tensor, "act": block.scalar, "dve": block.vector,
                      "pool": block.gpsimd, "sp": block.sync}
            for name, q in self.q.items():
                if not q.ops and not (name == "sp" and final_waits):
                    continue
                self._emit_q(engmap[name], q, final_waits if name == "sp" else ())

    def _emit_q(self, blockfn, q, final_waits):
        sched = self

        def body(e):
            waited = {}

            def wait(key, sem, val):
                if waited.get(key, 0) >= val:
                    return
                waited[key] = val
                e.wait_ge(sem, val)

            def wait_op(d):
                if d.dma_sem is not None:
                    wait(("d", d.dma_sem), sched.dma_sems[d.dma_sem], d.dma_val)
                else:
                    wait(("e", d.eng.name), d.eng.sem, d.semval)

            for o in q.ops:
                for d in o.deps:
                    wait_op(d)
                if o.dma_sem is not None:
                    if o.dma_prev is not None:
                        wait_op(o.dma_prev)
                    o.fn(e).then_inc(sched.dma_sems[o.dma_sem], 16)
                else:
                    inst = o.fn(e)
                    if o.is_target:
                        inst.then_inc(q.sem, 1)
            for d in final_waits:
                wait_op(d)

        blockfn(body)


def _col8(v):
    return np.ascontiguousarray(v.reshape(-1, 128).T)


class CLayout:
    def __init__(self):
        self.off = {}
        self.n = 0

    def add(self, name, width):
        self.off[name] = (self.n, width)
        self.n += width


CL = CLayout()
for _n in ("g_ffn1", "g_mix", "g_ffn2", "g_ple", "g_ssd", "cb", "lng", "lnb"):
    CL.add(_n, 8)
CL.add("cw", 8 * 31)
CL.add("sw", 12 * 4)
CL.add("sb", 12)
for _n in ("dtb", "alog", "dsk"):
    CL.add(_n, 16)
CL.add("eps", 1)
CL.add("one", 1)
CL.add("fin", 1024)


def pack_consts(inp):
    c = np.zeros((128, CL.n), np.float32)

    def put(name, arr):
        o, w = CL.off[name]
        c[:, o:o + w] = arr.reshape(128, w)
    put("g_ffn1", _col8(inp["ffn1_norm"][0]))
    put("g_mix", _col8(inp["mix_norm"][0]))
    put("g_ffn2", _col8(inp["ffn2_norm"][0]))
    put("g_ple", _col8(inp["ple_norm"][0]))
    put("g_ssd", _col8(inp["ssd_norm"][0]))
    put("cb", _col8(inp["conv_dw_b"][0]))
    put("lng", _col8(inp["conv_ln_g"][0]))
    put("lnb", _col8(inp["conv_ln_b"][0]))
    cw = inp["conv_dw_w"][0]
    put("cw", np.ascontiguousarray(cw.reshape(31, 8, 128).transpose(2, 1, 0)))
    sw = inp["ssd_conv_w"][0]
    put("sw", np.ascontiguousarray(sw.reshape(4, 12, 128).transpose(2, 1, 0)))
    put("sb", np.ascontiguousarray(inp["ssd_conv_b"][0].reshape(12, 128).T))
    put("dtb", np.broadcast_to(inp["ssd_dt_bias"][0][None, :], (128, 16)).copy())
    put("alog", np.broadcast_to(inp["ssd_A_log"][0][None, :], (128, 16)).copy())
    put("dsk", np.broadcast_to(inp["ssd_D"][0][None, :], (128, 16)).copy())
    put("eps", np.full((128, 1), EPS, np.float32))
    put("one", np.full((128, 1), 1.0, np.float32))
    put("fin", np.broadcast_to(inp["final_norm"][None, :], (128, 1024)).copy())
    return c


def make_masks():
    s = np.arange(128)[:, None]
    t = np.arange(128)[None, :]
    same = (s // 64) == (t // 64)
    m = np.zeros((128, 6, 128), np.float32)
    m[:, 0] = (s <= t)
    m[:, 1] = 1.0
    m[:, 2] = (s <= t) & same
    m[:, 3] = same
    m[:, 4] = (s < 64) & (t >= 0)
    m[:, 5] = (s >= 64) & (t >= 0)
    return m


WNAMES = ["ffn1_w_gate", "ffn1_w_up", "ffn1_w_down", "w_in", "w_out",
          "ffn2_w_gate", "ffn2_w_up", "ffn2_w_down", "ple_w_proj", "ple_w_gate"]


def build_program(NPRE, NFULL, SAMPLE=True):
    nc = bass.Bass("TRN2", target_bir_lowering=False)
    NT = NPRE + NFULL
    din = lambda name, shape: nc.dram_tensor(name, shape, F32, kind="ExternalInput").ap()
    dout = lambda name, shape: nc.dram_tensor(name, shape, F32, kind="ExternalOutput").ap()
    xin = din("xin", [max(NT, 1) * TT, D])
    pin = din("pin", [max(NFULL, 1) * TT, 256])
    flags = din("flags", [128, max(NT, 1)])
    xs_in = din("xs_in", [128, D])
    ps_in = din("ps_in", [128, 256])
    sconv = din("sconv", [2, 30, 1024])
    sxbc = din("sxbc", [2, 3, 1536])
    sssd = din("sssd", [2, 1024, 128])
    consts_d = din("consts", [128, CL.n])
    masks_d = din("masks", [128, 6, 128])
    W = {}
    wshape = {"ffn1_w_gate": [D, DFF], "ffn1_w_up": [D, DFF], "ffn1_w_down": [DFF, D], "w_in": [D, INP],
              "w_out": [2048, D], "ffn2_w_gate": [D, DFF], "ffn2_w_up": [D, DFF], "ffn2_w_down": [DFF, D],
              "ple_w_proj": [256, D], "ple_w_gate": [D, D]}
    for n in WNAMES:
        W[n] = din(n, wshape[n])
    yp = dout("yp", [max(NFULL, 1) * TT, D])
    ys = dout("ys", [128, D])
    o_conv_p = dout("o_conv_p", [30, 1024])
    o_xbc_p = dout("o_xbc_p", [3, 1536])
    o_ssd_p = dout("o_ssd_p", [1024, 128])
    o_conv_s = dout("o_conv_s", [2, 30, 1024])
    o_xbc_s = dout("o_xbc_s", [2, 3, 1536])
    o_ssd_s = dout("o_ssd_s", [2, 1024, 128])

    S = Sched(nc)
    es = contextlib.ExitStack()
    sbt = lambda name, shape, dt: es.enter_context(nc.sbuf_tensor(name, shape, dt))
    pst = lambda name, shape, dt: es.enter_context(nc.psum_tensor(name, shape, dt))
    out_dmas = []

    with es:
        cst = sbt("cst", [128, CL.n], F32)
        msk = sbt("msk", [128, 6, 128], F32)
        flg = sbt("flg", [128, max(NT, 1)], F32)
        ident_b = sbt("ident_b", [128, 128], BF16)
        ident_f = sbt("ident_f", [128, 128], F32)
        abc = sbt("abc", [128, 16], F32)
        h = sbt("h", [128, 4, D], F32)
        xsbf = sbt("xsbf", [128, 4, D], BF16)
        nst = sbt("nst", [128, 4, 4], F32)
        xnT = sbt("xnT", [128, 8, TT], BF16)
        regH = sbt("regH", [128, NFC * TT], BF16)
        sg = sbt("sg", [128, 2, TT], F32)
        RS = 5
        ring = [sbt(f"ring{i}", [128, 8, 256], BF16) for i in range(RS)]
        WDq = [sbt(f"wdq{i}", [128, NFC, 256], BF16) for i in range(2)]
        regM = sbt("regM", [128, 12 * (TT + 3) + 4], F32)
        regA = sbt("regA", [128, 8 * (TT + 30)], BF16)
        a_hist = sbt("a_hist", [128, 8, 30], BF16)
        x_hist = sbt("x_hist", [128, 12, 3], BF16)
        a_last = sbt("a_last", [128, 2, 8, 32], F32)
        x_last = sbt("x_last", [128, 2, 12, 4], F32)
        Rt = sbt("Rt", [128, 16, 128], F32)
        cbm = sbt("cbm", [128, 2, 128], F32)
        x_tok = sbt("x_tok", [128, D], BF16)
        B_tok = sbt("B_tok", [128, 256], BF16)
        xw2 = sbt("xw2", [128, 2, D], BF16)
        bcT_p = sbt("bcT_p", [128, 4, TT], BF16)
        xw = xw2[:, 0, :]
        yx = sbt("yx", [128, 2 * D], F32)
        ybuf = yx[:, 0:D]
        xd = yx[:, D:2 * D]
        stg = yx[:, 0:1536]
        CTm = sbt("CTm", [128, 2, 2, 128], BF16)
        St = [sbt(f"St{i}", [128, D], F32) for i in range(2)]
        Sb = [sbt(f"Sb{i}", [128, D], BF16) for i in range(2)]
        sm = sbt("sm", [128, 4, 160], F32)
        pbuf = sbt("pbuf", [128, 256], F32)
        pb16 = sbt("pb16", [128, 256], BF16)
        YW = sbt("YW", [128, 8 * TT + 16 * 128], BF16)
        yTall = YW[:, 0:8 * TT].rearrange("p (k t) -> p k t", t=TT)
        wT = YW[:, 8 * TT:8 * TT + 2048].rearrange("p (j t) -> p j t", t=128)
        dgx = YW[:].rearrange("p (k t) -> p k t", t=128)
        pTall = sbt("pTall", [128, 2, TT], BF16)

        PB = pst("PB", [128, 6 * 512], F32)
        PTb = [pst(f"PT{i}", [128, 8, 128], BF16) for i in range(2)]
        bank = lambda i: PB[:, i * 512:(i + 1) * 512]
        bb = [Buf(f"bank{i}") for i in range(6)]
        ptb = [Buf(f"pt{i}") for i in range(2)]

        def C(name):
            o, w = CL.off[name]
            return cst[:, o:o + w]
        b_cst, b_msk, b_flg, b_idb, b_idf, b_abc = (Buf(n) for n in ("cst", "msk", "flg", "idb", "idf", "abc"))
        hb = [Buf(f"h{c}") for c in range(4)]
        xsb = [Buf(f"xsbf{c}") for c in range(4)]
        nsb = [Buf(f"nst{c}") for c in range(4)]
        xnTb = [Buf(f"xnT{c}") for c in range(4)]
        sgb = [Buf("sg0"), Buf("sg1")]
        ringb = [Buf(f"ring{i}") for i in range(RS)]
        wdb = [[Buf(f"wd{q}_{p}") for p in range(3)] for q in range(2)]
        smb = [Buf(f"sm{c}") for c in range(4)]
        b_ahist, b_xhist, b_alast, b_xlast = Buf("ahist"), Buf("xhist"), Buf("alast"), Buf("xlast")
        b_wT, b_cbm, b_xtok, b_Btok, b_xw, b_ybuf, b_xd, b_CTm = (
            Buf(n) for n in ("wT", "cbm", "xtok", "Btok", "xw", "ybuf", "xd", "CTm"))
        wTb = [b_wT]
        xwb = [b_xw, Buf("xw1")]
        yTb = [Buf(f"yT{c}") for c in range(4)]
        pTb = [Buf(f"pT{c}") for c in range(4)]
        Stb = [Buf("St0"), Buf("St1")]
        Sbb = [Buf("Sb0"), Buf("Sb1")]
        b_pbuf, b_pb16, b_stg = Buf("pbuf"), Buf("pb16"), Buf("stg")

        class Region:
            def __init__(self):
                self.bufs = []

            def phase(self, names):
                fence = []
                for b in self.bufs:
                    if b.last_w is not None:
                        fence.append(b.last_w)
                    fence.extend(b.readers)
                self.bufs = [Buf(n, fence) for n in names]
                return self.bufs
        RH, RM, RA, RR, RYW, RYX, RB = (Region() for _ in range(7))
        xsT_p = yx[:].bitcast(BF16).rearrange("p (k t) -> p k t", t=TT)
        dg = Rt[:].rearrange("p j t -> p (j t)").bitcast(BF16)[:, 0:31 * 128].rearrange("p (k t) -> p k t", t=128)

        SS, RSTD, TMP1 = 0, 1, 2
        DT0, DTA0, ACOL, AEND, NACOL, EAC, WEND, EEND0, EEND1, TMP16 = 16, 32, 48, 64, 80, 96, 112, 128, 144, 0

        wait_all = []

        S.dma("sp", lambda e: e.dma_start(out=cst[:], in_=consts_d), writes=[b_cst])
        S.dma("sp", lambda e: e.dma_start(out=msk[:], in_=masks_d), writes=[b_msk])
        S.dma("sp", lambda e: e.dma_start(out=flg[:], in_=flags), writes=[b_flg])

        for t_, b_ in ((ident_b, b_idb), (ident_f, b_idf)):
            S.op("pool", lambda e, t_=t_: e.memset(t_[:], 0.0), writes=[b_])
            S.op("pool", lambda e, t_=t_: e.affine_select(out=t_[:], in_=t_[:], pattern=[[-1, 128]], compare_op=ALU.not_equal,
                                                          fill=1.0, base=0, channel_multiplier=1),
                 reads=[b_], writes=[b_])
        S.op("act", lambda e: e.activation(out=abc[:], in_=C("alog"), func=AF.Exp), reads=[b_cst], writes=[b_abc])
        S.op("dve", lambda e: e.tensor_scalar(out=abc[:], in0=abc[:], scalar1=-1.0, scalar2=None, op0=ALU.mult),
             reads=[b_abc], writes=[b_abc])
        U_p, E_p, U_s, EB_s, E_s = msk[:, 0, :], msk[:, 1, :], msk[:, 2, :], msk[:, 3, :], [msk[:, 4, :], msk[:, 5, :]]

        ring_pos = [0]

        def ring_load(w_ap, ncols):
            i = ring_pos[0] % RS
            ring_pos[0] += 1
            S.dma("pool", lambda e: e.dma_start(out=ring[i][:, :, 0:ncols],
                                                in_=w_ap.rearrange("(kc p) n -> p kc n", p=128)),
                  writes=[ringb[i]])
            return ring[i], ringb[i]

        def wd_load(q, w_ap, nk, kofs=0):
            for p0 in range(0, nk, 8):
                pn = min(8, nk - p0)
                S.dma("pool", lambda e, p0=p0, pn=pn: e.dma_start(
                    out=WDq[q][:, kofs + p0:kofs + p0 + pn, :],
                    in_=w_ap[p0 * 128:(p0 + pn) * 128, :].rearrange("(kc p) n -> p kc n", p=128)),
                    writes=[wdb[q][(kofs + p0) // 8]])

        def rstd_stage(srcs, cs, width=D):
            for (ap, bufs), c in zip(srcs, cs):
                S.op("dve", lambda e, c=c: e.memset(nst[:, c, SS:SS + 1], 0.0), writes=[nsb[c]])
            for (ap, bufs), c in zip(srcs, cs):
                S.op("act", lambda e, c=c, ap=ap: e.activation(out=xsbf[:, c, 0:width], in_=ap, func=AF.Square,
                                                               accum_out=nst[:, c, SS:SS + 1]),
                     reads=list(bufs), writes=[nsb[c], xsb[c]])
            for (ap, bufs), c in zip(srcs, cs):
                S.op("act", lambda e, c=c: e.activation(out=nst[:, c, TMP1:TMP1 + 1], in_=nst[:, c, SS:SS + 1], func=AF.Sqrt,
                                                        bias=C("eps"), scale=1.0 / width),
                     reads=[nsb[c], b_cst], writes=[nsb[c]])
            for (ap, bufs), c in zip(srcs, cs):
                S.op("dve", lambda e, c=c: e.reciprocal(out=nst[:, c, RSTD:RSTD + 1], in_=nst[:, c, TMP1:TMP1 + 1]),
                     reads=[nsb[c]], writes=[nsb[c]])

        def rstd_of(src_ap, src_bufs, c, width=D):
            rstd_stage([(src_ap, src_bufs)], [c], width)

        xnTB = Rt[:].rearrange("p j t -> p (j t)").bitcast(BF16).rearrange("p (k t) -> p k t", t=TT)
        xnBb = [None] * 4

        def norms(nch, gname, sel="A", part=0):
            cs = list(range(nch))
            if part in (0, 1):
                rstd_stage([(h[:, c, :], [hb[c]]) for c in cs], cs)
                for c in cs:
                    if part == 1:
                        S.op("dve", lambda e, c=c: e.tensor_scalar(out=xsbf[:, c, :], in0=h[:, c, :], scalar1=nst[:, c, RSTD:RSTD + 1],
                                                                   scalar2=None, op0=ALU.mult),
                             reads=[hb[c], nsb[c]], writes=[xsb[c]])
                    else:
                        S.op("act", lambda e, c=c: e.activation(out=xsbf[:, c, :], in_=h[:, c, :], func=AF.Identity,
                                                                scale=nst[:, c, RSTD:RSTD + 1]),
                             reads=[hb[c], nsb[c]], writes=[xsb[c]])
            if part in (0, 2):
                if sel == "B":
                    xnBb[:] = RR.phase([f"xnB{c}" for c in range(4)])
                XT, XTb = (xnT, xnTb) if sel == "A" else (xnTB, xnBb)
                for c0 in range(0, nch, 2):
                    for c in cs[c0:c0 + 2]:
                        transpose_only(xsbf[:, c, :], xsb[c], c % 2)
                    for c in cs[c0:c0 + 2]:
                        scale_evac(XT[:, :, c * 128:(c + 1) * 128], XTb[c], gname, c % 2)

        def transpose_only(src, srcb, pti):
            def tr(e):
                for k in range(8):
                    i = e.transpose(out=PTb[pti][:, k, :], in_=src[:, k * 128:(k + 1) * 128], identity=ident_b[:])
                return i
            S.op("pe", tr, reads=[srcb, b_idb], writes=[ptb[pti]])

        def scale_evac(dst, dstb, gname, pti):
            g = C(gname).unsqueeze(2).broadcast_to([128, 8, 128])
            S.op("dve", lambda e: e.tensor_tensor(out=dst, in0=PTb[pti][:], in1=g, op=ALU.mult),
                 reads=[ptb[pti], b_cst], writes=[dstb])

        def transpose_scale(src, srcb, dst, dstb, gname, pti):
            def tr(e):
                for k in range(8):
                    i = e.transpose(out=PTb[pti][:, k, :], in_=src[:, k * 128:(k + 1) * 128], identity=ident_b[:])
                return i
            S.op("pe", tr, reads=[srcb, b_idb], writes=[ptb[pti]])
            g = C(gname).unsqueeze(2).broadcast_to([128, 8, 128])
            S.op("dve", lambda e: e.tensor_tensor(out=dst, in0=PTb[pti][:], in1=g, op=ALU.mult),
                 reads=[ptb[pti], b_cst], writes=[dstb])

        def ffn(nch, wg, wu, wd, gname, do_norm=True, hooks=None, sel="A"):
            n = nch * 128
            hid = regH[:].rearrange("p (f t) -> p f t", t=TT)
            hidb = RH.phase([f"hid{f}" for f in range(NFC)])
            if do_norm:
                norms(nch, gname, sel)
            XT, XTb = (xnT, xnTb) if sel == "A" else (xnTB, xnBb)
            xr = [XTb[c] for c in range(nch)]
            for u in range(NFC // 2):
                if hooks and u in hooks:
                    hooks[u]()
                tg, tgb = ring_load(wg[:, u * 256:(u + 1) * 256], 256)
                tu, tub = ring_load(wu[:, u * 256:(u + 1) * 256], 256)
                for fi in range(2):
                    f = 2 * u + fi
                    pg, pu = bank(f % 2), bank(2 + f % 2)

                    def mmg(e, t=tg, fi=fi, p=pg):
                        for k in range(8):
                            i = e.matmul(p[:, 0:n], lhsT=t[:, k, fi * 128:(fi + 1) * 128], rhs=XT[:, k, 0:n],
                                         start=(k == 0), stop=(k == 7))
                        return i
                    S.op("pe", mmg, reads=[tgb] + xr, writes=[bb[f % 2]])
                    S.op("pe", lambda e, t=tu, fi=fi, p=pu, mmg=mmg: mmg(e, t, fi, p), reads=[tub] + xr, writes=[bb[2 + f % 2]])
                    S.op("act", lambda e, f=f, p=pg: e.activation(out=sg[:, f % 2, 0:n], in_=p[:, 0:n], func=AF.Silu),
                         reads=[bb[f % 2]], writes=[sgb[f % 2]])
                    S.op("dve", lambda e, f=f, p=pu: e.tensor_tensor(out=hid[:, f, 0:n], in0=sg[:, f % 2, 0:n],
                                                                     in1=p[:, 0:n], op=ALU.mult),
                         reads=[sgb[f % 2], bb[2 + f % 2]], writes=[hidb[f]])
            for qd in range(4):
                wd_load(qd % 2, wd[:, qd * 256:(qd + 1) * 256], NFC)
                for c in range(nch):
                    po, pob = bank(4 + c % 2), bb[4 + c % 2]

                    def mmd(e, c=c, qd=qd, po=po):
                        for f in range(NFC):
                            i = e.matmul(po[:, 0:256], lhsT=hid[:, f, c * 128:(c + 1) * 128], rhs=WDq[qd % 2][:, f, :],
                                         start=(f == 0), stop=(f == NFC - 1))
                        return i
                    S.op("pe", mmd, reads=hidb + wdb[qd % 2], writes=[pob])
                    hs = h[:, c, qd * 256:(qd + 1) * 256]
                    S.op("dve", lambda e, po=po, hs=hs: e.scalar_tensor_tensor(out=hs, in0=po[:, 0:256], scalar=0.5,
                                                                                in1=hs, op0=ALU.mult, op1=ALU.add),
                         reads=[pob, hb[c]], writes=[hb[c]])

        def mixer(nch, segs, mode, fcol, first, nseq, smp, hook_after_norm=None, hook_after_ssd1=None, need_last=True):
            n = nch * 128
            w_in = W["w_in"]
            norms(nch, "g_mix")
            if hook_after_norm is not None:
                hook_after_norm()
            xr = [xnTb[c] for c in range(nch)]
            cT = regH[:, 0:8 * TT].rearrange("p (f t) -> p f t", t=TT)
            xbcT = regH[:, 8 * TT:20 * TT].rearrange("p (f t) -> p f t", t=TT)
            do_conf = mode in ("full", "prefix_last")
            full = mode == "full"
            if full:
                rh = RH.phase([f"cT{i}" for i in range(8)] + [f"xbcT{i}" for i in range(12)])
                cTb, xbcTb = rh[0:8], rh[8:20]
                XB = lambda xc: xbcT[:, xc, :]
            else:
                xbcTb = RYX.phase([f"xsTp{i}" for i in range(8)]) + RB.phase([f"bcTp{i}" for i in range(4)])
                XB = lambda xc: (xsT_p[:, xc, :] if xc < 8 else bcT_p[:, xc - 8, :])
            AW = 30 + TT if not smp else 2 * (30 + 64)
            a_pad = regA[:, 0:8 * AW].rearrange("p (c t) -> p c t", t=AW)
            XW = 3 + TT if not smp else 2 * (3 + 64)
            xbcpre = regM[:].bitcast(BF16)[:, 0:12 * XW].rearrange("p (c t) -> p c t", t=XW)

            def P0(si):
                return si * (30 + 64) if smp else 0

            def X0(si):
                return si * (3 + 64) if smp else 0

            if do_conf:
                apb = RA.phase([f"apad{i}" for i in range(8)])
                convy = regM[:, 0:8 * TT].rearrange("p (c t) -> p c t", t=TT)
                ysq = regM[:, 8 * TT:10 * TT].rearrange("p (a t) -> p a t", t=TT)
                lnm = regM[:, 10 * TT:11 * TT]
                lnr = regM[:, 11 * TT:12 * TT]
                rm = RM.phase([f"convy{i}" for i in range(8)] + ["ysq0", "ysq1", "lnm", "lnr"])
                convyb, b_ysq, b_lnm, b_lnr = rm[0:8], rm[8:10], rm[10], rm[11]
                if not smp:
                    if first or mode == "prefix_last":
                        S.op("dve", lambda e: e.memset(a_pad[:, :, 0:30], 0.0), writes=apb)
                    else:
                        S.op("act", lambda e: e.activation(out=a_pad[:, :, 0:30], in_=a_hist[:], func=AF.Copy),
                             reads=[b_ahist], writes=apb)
                else:
                    for si in range(2):
                        load_hist_T(sconv[si], 30, 8, a_pad, P0(si), apb)
                for u in range(4):
                    tv, tvb = ring_load(w_in[:, u * 256:(u + 1) * 256], 256)
                    tg, tgb = ring_load(w_in[:, 1024 + u * 256:1024 + (u + 1) * 256], 256)
                    for fi in range(2):
                        ch = 2 * u + fi
                        pv, pg = bank(ch % 2), bank(2 + ch % 2)

                        def mm(e, t=tv, fi=fi, p=pv):
                            for k in range(8):
                                i = e.matmul(p[:, 0:n], lhsT=t[:, k, fi * 128:(fi + 1) * 128], rhs=xnT[:, k, 0:n],
                                             start=(k == 0), stop=(k == 7))
                            return i
                        S.op("pe", mm, reads=[tvb] + xr, writes=[bb[ch % 2]])
                        S.op("pe", lambda e, t=tg, fi=fi, p=pg, mm=mm: mm(e, t, fi, p), reads=[tgb] + xr, writes=[bb[2 + ch % 2]])
                        S.op("act", lambda e, ch=ch, p=pg: e.activation(out=sg[:, ch % 2, 0:n], in_=p[:, 0:n],
                                                                        func=AF.Sigmoid),
                             reads=[bb[2 + ch % 2]], writes=[sgb[ch % 2]])
                        for si, (lo, sn) in enumerate(segs):
                            S.op("dve", lambda e, ch=ch, p=pv, lo=lo, sn=sn, si=si: e.tensor_tensor(
                                out=a_pad[:, ch, P0(si) + 30:P0(si) + 30 + sn], in0=p[:, lo:lo + sn],
                                in1=sg[:, ch % 2, lo:lo + sn], op=ALU.mult),
                                reads=[bb[ch % 2], sgb[ch % 2]], writes=[apb[ch]])
                            if need_last:
                                S.op("dve", lambda e, ch=ch, p=pv, lo=lo, sn=sn, si=si: e.tensor_tensor(
                                    out=a_last[:, si, ch, :], in0=p[:, lo + sn - 32:lo + sn],
                                    in1=sg[:, ch % 2, lo + sn - 32:lo + sn], op=ALU.mult),
                                    reads=[bb[ch % 2], sgb[ch % 2]], writes=[b_alast])
                if not smp:
                    if fcol is not None:
                        S.op("act", lambda e: e.activation(out=a_hist[:], in_=a_pad[:, :, TT:TT + 30], func=AF.Identity,
                                                           scale=fcol),
                             reads=apb + [b_flg], writes=[b_ahist])
                    else:
                        S.op("act", lambda e: e.activation(out=a_hist[:], in_=a_pad[:, :, TT:TT + 30], func=AF.Copy),
                             reads=apb, writes=[b_ahist])
            if full:
                cw = C("cw").rearrange("p (c k) -> p c k", k=31)
                stats_pending = []
                b_dgA, b_dgB = RR.phase(["dgA", "dgB"])
                for ch in range(8):
                    S.op("dve", lambda e, ch=ch: e.tensor_tensor(
                        out=dg[:, 0:16, :], in0=ident_b[:].unsqueeze(1).broadcast_to([128, 16, 128]),
                        in1=cw[:, ch, 0:16].unsqueeze(2).broadcast_to([128, 16, 128]), op=ALU.mult),
                        reads=[b_idb, b_cst], writes=[b_dgA])
                    S.op("dve", lambda e, ch=ch: e.tensor_tensor(
                        out=dg[:, 16:31, :], in0=ident_b[:].unsqueeze(1).broadcast_to([128, 15, 128]),
                        in1=cw[:, ch, 16:31].unsqueeze(2).broadcast_to([128, 15, 128]), op=ALU.mult),
                        reads=[b_idb, b_cst], writes=[b_dgB])
                    pc, pcb = bank(4 + ch % 2), bb[4 + ch % 2]

                    def cv(e, ch=ch, pc=pc, k0=0, k1=16):
                        for si, (lo, sn) in enumerate(segs):
                            for k in range(k0, k1):
                                i = e.matmul(pc[:, lo:lo + sn], lhsT=dg[:, k, :],
                                             rhs=a_pad[:, ch, P0(si) + k:P0(si) + k + sn],
                                             start=(k == 0), stop=(k == 30))
                        return i
                    if len(segs) == 1:
                        S.op("pe", cv, reads=[b_dgA, apb[ch]], writes=[pcb])
                        S.op("pe", lambda e, ch=ch, pc=pc, cv=cv: cv(e, ch, pc, 16, 31), reads=[b_dgB, apb[ch]], writes=[pcb])
                    else:
                        S.op("pe", lambda e, ch=ch, pc=pc, cv=cv: cv(e, ch, pc, 0, 31), reads=[b_dgA, b_dgB, apb[ch]], writes=[pcb])
                    while len(stats_pending) > 0 and stats_pending[0] < ch:
                        stats(stats_pending.pop(0))
                    cbo = CL.off["cb"][0]
                    S.op("act", lambda e, ch=ch, pc=pc: e.activation(out=convy[:, ch, 0:n], in_=pc[:, 0:n], func=AF.Identity,
                                                                     bias=cst[:, cbo + ch:cbo + ch + 1]),
                         reads=[pcb, b_cst], writes=[convyb[ch]])
                    S.op("act", lambda e, ch=ch: e.activation(out=ysq[:, ch % 2, 0:n], in_=convy[:, ch, 0:n], func=AF.Square),
                         reads=[convyb[ch]], writes=[b_ysq[ch % 2]])
                    def stats(ch):
                        S.op("pe", lambda e, ch=ch: e.matmul(bank(0)[:, 0:n], lhsT=E_p, rhs=convy[:, ch, 0:n],
                                                             start=(ch == 0), stop=(ch == 7)),
                             reads=[b_msk, convyb[ch]], writes=[bb[0]])
                        S.op("pe", lambda e, ch=ch: e.matmul(bank(1)[:, 0:n], lhsT=E_p, rhs=ysq[:, ch % 2, 0:n],
                                                             start=(ch == 0), stop=(ch == 7)),
                             reads=[b_msk, b_ysq[ch % 2]], writes=[bb[1]])
                    stats_pending.append(ch)
                    if not STATS_DELAY:
                        stats(stats_pending.pop(0))
                while stats_pending:
                    stats(stats_pending.pop(0))
                S.op("dve", lambda e: e.tensor_scalar(out=lnm[:, 0:n], in0=bank(0)[:, 0:n], scalar1=1.0 / 1024, scalar2=None,
                                                      op0=ALU.mult), reads=[bb[0]], writes=[b_lnm])
                S.op("dve", lambda e: e.tensor_tensor(out=lnr[:, 0:n], in0=lnm[:, 0:n], in1=lnm[:, 0:n], op=ALU.mult),
                     reads=[b_lnm], writes=[b_lnr])
                S.op("dve", lambda e: e.scalar_tensor_tensor(out=lnr[:, 0:n], in0=bank(1)[:, 0:n], scalar=1.0 / 1024,
                                                             in1=lnr[:, 0:n], op0=ALU.mult, op1=ALU.subtract),
                     reads=[bb[1], b_lnr], writes=[b_lnr])
                S.op("act", lambda e: e.activation(out=lnr[:, 0:n], in_=lnr[:, 0:n], func=AF.Sqrt, bias=C("eps")),
                     reads=[b_lnr, b_cst], writes=[b_lnr])
                S.op("dve", lambda e: e.reciprocal(out=lnr[:, 0:n], in_=lnr[:, 0:n]), reads=[b_lnr], writes=[b_lnr])
                lgo, lbo = CL.off["lng"][0], CL.off["lnb"][0]
                for ch in range(8):
                    S.op("dve", lambda e, ch=ch: e.tensor_tensor(out=convy[:, ch, 0:n], in0=convy[:, ch, 0:n],
                                                                 in1=lnm[:, 0:n], op=ALU.subtract),
                         reads=[convyb[ch], b_lnm], writes=[convyb[ch]])
                    S.op("dve", lambda e, ch=ch: e.tensor_tensor(out=convy[:, ch, 0:n], in0=convy[:, ch, 0:n],
                                                                 in1=lnr[:, 0:n], op=ALU.mult),
                         reads=[convyb[ch], b_lnr], writes=[convyb[ch]])
                    S.op("act", lambda e, ch=ch: e.activation(out=cT[:, ch, 0:n], in_=convy[:, ch, 0:n], func=AF.Silu,
                                                              bias=cst[:, lbo + ch:lbo + ch + 1],
                                                              scale=cst[:, lgo + ch:lgo + ch + 1]),
                         reads=[convyb[ch], b_cst], writes=[cTb[ch]])

            rm2 = RM.phase([f"xpre{i}" for i in range(12)])
            xpb = rm2
            if not smp:
                if first:
                    S.op("dve", lambda e: e.memset(xbcpre[:, :, 0:3], 0.0), writes=xpb)
                else:
                    S.op("act", lambda e: e.activation(out=xbcpre[:, :, 0:3], in_=x_hist[:], func=AF.Copy),
                         reads=[b_xhist], writes=xpb)
            else:
                for si in range(2):
                    load_hist_T(sxbc[si], 3, 12, xbcpre, X0(si), xpb)
            swv = C("sw").rearrange("p (c k) -> p c k", k=4)
            sbo = CL.off["sb"][0]
            b_dgx, = RYW.phase(["dgx"])
            for xc in range(12):
                S.op("dve", lambda e, xc=xc: e.tensor_tensor(
                    out=dgx[:, 4 * xc:4 * xc + 4, :], in0=ident_b[:].unsqueeze(1).broadcast_to([128, 4, 128]),
                    in1=swv[:, xc, :].unsqueeze(2).broadcast_to([128, 4, 128]), op=ALU.mult),
                    reads=[b_idb, b_cst], writes=[b_dgx])

            def short_conv(xc):
                pc, pcb = bank(4 + xc % 2), bb[4 + xc % 2]

                def cvx(e, xc=xc, pc=pc):
                    for si, (lo, sn) in enumerate(segs):
                        for k in range(4):
                            i = e.matmul(pc[:, lo:lo + sn], lhsT=dgx[:, 4 * xc + k, :],
                                         rhs=xbcpre[:, xc, X0(si) + k:X0(si) + k + sn], start=(k == 0), stop=(k == 3))
                    return i
                S.op("pe", cvx, reads=[b_dgx, xpb[xc]], writes=[pcb])
                S.op("act", lambda e, xc=xc, pc=pc: e.activation(out=XB(xc)[:, 0:n], in_=pc[:, 0:n], func=AF.Silu,
                                                                 bias=cst[:, sbo + xc:sbo + xc + 1]),
                     reads=[pcb, b_cst], writes=[xbcTb[xc]])

            pend = []
            for u in range(6):
                if u == 1 and hook_after_ssd1 is not None:
                    hook_after_ssd1(1)
                if u == 4 and hook_after_ssd1 is not None:
                    hook_after_ssd1(2)
                tx, txb = ring_load(w_in[:, 3072 + u * 256:3072 + (u + 1) * 256], 256)
                for fi in range(2):
                    xc = 2 * u + fi
                    px, pxb = bank(xc % 4), bb[xc % 4]

                    def mm(e, t=tx, fi=fi, p=px):
                        for k in range(8):
                            i = e.matmul(p[:, 0:n], lhsT=t[:, k, fi * 128:(fi + 1) * 128], rhs=xnT[:, k, 0:n],
                                         start=(k == 0), stop=(k == 7))
                        return i
                    S.op("pe", mm, reads=[txb] + xr, writes=[pxb])
                    for si, (lo, sn) in enumerate(segs):
                        S.op("act", lambda e, xc=xc, p=px, lo=lo, sn=sn, si=si: e.activation(
                            out=xbcpre[:, xc, X0(si) + 3:X0(si) + 3 + sn], in_=p[:, lo:lo + sn], func=AF.Copy),
                            reads=[pxb], writes=[xpb[xc]])
                        if need_last:
                            S.op("act", lambda e, xc=xc, p=px, lo=lo, sn=sn, si=si: e.activation(
                                out=x_last[:, si, xc, :], in_=p[:, lo + sn - 4:lo + sn], func=AF.Copy),
                                reads=[pxb], writes=[b_xlast])
                    pend.append(xc)
                    if len(pend) > 2:
                        short_conv(pend.pop(0))
            while pend:
                short_conv(pend.pop(0))
            if not smp:
                if fcol is not None:
                    S.op("act", lambda e: e.activation(out=x_hist[:], in_=xbcpre[:, :, TT:TT + 3], func=AF.Identity, scale=fcol),
                         reads=xpb + [b_flg], writes=[b_xhist])
                else:
                    S.op("act", lambda e: e.activation(out=x_hist[:], in_=xbcpre[:, :, TT:TT + 3], func=AF.Copy),
                         reads=xpb, writes=[b_xhist])

            def do_dt():
                tdt, tdtb = ring_load(w_in[:, 4608:4624], 16)
                for c in range(nch):
                    pdt = bank(4)[:, c * 16:(c + 1) * 16]

                    def mm(e, c=c, pdt=pdt):
                        for k in range(8):
                            i = e.matmul(pdt, lhsT=xnT[:, k, c * 128:(c + 1) * 128], rhs=tdt[:, k, 0:16],
                                         start=(k == 0), stop=(k == 7))
                        return i
                    S.op("pe", mm, reads=[tdtb, xnTb[c]], writes=[bb[4]])
                for c in range(nch):
                    S.op("dve", lambda e, c=c: e.tensor_tensor(out=sm[:, c, DT0:DT0 + 16], in0=bank(4)[:, c * 16:(c + 1) * 16],
                                                               in1=C("dtb"), op=ALU.add),
                         reads=[bb[4], b_cst], writes=[smb[c]])
                for c in range(nch):
                    S.op("act", lambda e, c=c: e.activation(out=sm[:, c, DT0:DT0 + 16], in_=sm[:, c, DT0:DT0 + 16], func=AF.Exp),
                         reads=[smb[c]], writes=[smb[c]])
                for c in range(nch):
                    S.op("act", lambda e, c=c: e.activation(out=sm[:, c, DT0:DT0 + 16], in_=sm[:, c, DT0:DT0 + 16], func=AF.Ln,
                                                            bias=C("one")),
                         reads=[smb[c], b_cst], writes=[smb[c]])
                for c in range(nch):
                    S.op("dve", lambda e, c=c: e.tensor_tensor(out=sm[:, c, DTA0:DTA0 + 16], in0=sm[:, c, DT0:DT0 + 16],
                                                               in1=abc[:], op=ALU.mult),
                         reads=[smb[c], b_abc], writes=[smb[c]])

            if not full:
                holder = [None]

                do_dt()
                PREFIX_DEFER = ssd_prefix(nch, XB, xbcTb, fcol, None)
            else:
                do_dt()

            if full:
                sz = regM[:, 0:4 * D].rearrange("p (c d) -> p c d", d=D)
                szb = RM.phase([f"sz{c}" for c in range(4)])
                zu = [ring_load(w_in[:, 2048 + u * 256:2048 + (u + 1) * 256], 256) for u in range(4)]
                for c in range(nch):
                    for hh in range(2):
                        pz, pzb = bank(hh), bb[hh]

                        def mm(e, c=c, hh=hh, pz=pz):
                            for uu in range(2):
                                t = zu[2 * hh + uu][0]
                                for k in range(8):
                                    i = e.matmul(pz[:, uu * 256:(uu + 1) * 256], lhsT=xnT[:, k, c * 128:(c + 1) * 128],
                                                 rhs=t[:, k, :], start=(k == 0), stop=(k == 7))
                            return i
                        S.op("pe", mm, reads=[zu[2 * hh][1], zu[2 * hh + 1][1], xnTb[c]], writes=[pzb])
                        S.op("act", lambda e, c=c, hh=hh, pz=pz: e.activation(out=sz[:, c, hh * 512:(hh + 1) * 512], in_=pz,
                                                                              func=AF.Silu),
                             reads=[pzb], writes=[szb[c]])

            if not full:
                return PREFIX_DEFER

            Um, EBm = (U_s, EB_s) if smp else (U_p, E_p)
            b_R, = RR.phase(["R"])
            _b = RYW.phase([f"yT{c}" for c in range(4)] + ["wT"])
            yTb[:] = _b[0:4]
            wTb[0] = _b[4]
            for c in range(nch):
                smc = sm[:, c, :]
                cr = slice(c * 128, (c + 1) * 128)
                def trx(e, cr=cr):
                    for j in range(8):
                        i = e.transpose(out=PTb[0][:, j, :], in_=XB(j)[:, cr], identity=ident_b[:])
                    return i
                S.op("pe", trx, reads=xbcTb[0:8] + [b_idb], writes=[ptb[0]])
                S.op("act", lambda e: e.activation(out=x_tok[:], in_=PTb[0][:].rearrange("p a b -> p (a b)"), func=AF.Copy),
                     reads=[ptb[0]], writes=[b_xtok])

                def trb(e, cr=cr):
                    for g in range(2):
                        i = e.transpose(out=PTb[1][:, g, :], in_=XB(8 + g)[:, cr], identity=ident_b[:])
                    return i
                S.op("pe", trb, reads=xbcTb[8:10] + [b_idb], writes=[ptb[1]])
                S.op("act", lambda e: e.activation(out=B_tok[:], in_=PTb[1][:, 0:2, :].rearrange("p a b -> p (a b)"),
                                                   func=AF.Copy),
                     reads=[ptb[1]], writes=[b_Btok])
                pac = bank(5)

                def mac(e, smc=smc):
                    e.matmul(pac[:, 0:16], lhsT=Um, rhs=smc[:, DTA0:DTA0 + 16], start=True, stop=True)
                    i = e.matmul(pac[:, 16:32], lhsT=EBm, rhs=smc[:, DTA0:DTA0 + 16], start=True, stop=True)
                    if smp:
                        for q in range(2):
                            i = e.matmul(pac[:, 32 + 16 * q:48 + 16 * q], lhsT=E_s[q], rhs=smc[:, DTA0:DTA0 + 16],
                                         start=True, stop=True)
                    return i
                S.op("pe", mac, reads=[b_msk, smb[c]], writes=[bb[5]])
                S.op("dve", lambda e, smc=smc: e.tensor_copy(out=smc[:, ACOL:ACOL + 32], in_=pac[:, 0:32]),
                     reads=[bb[5]], writes=[smb[c]])
                if smp:
                    S.op("dve", lambda e, smc=smc: e.tensor_copy(out=smc[:, EEND0:EEND0 + 32], in_=pac[:, 32:64]),
                         reads=[bb[5]], writes=[smb[c]])
                else:
                    S.op("dve", lambda e, smc=smc: e.tensor_copy(out=smc[:, EEND0:EEND0 + 16], in_=pac[:, 16:32]),
                         reads=[bb[5]], writes=[smb[c]])
                S.op("dve", lambda e, smc=smc: e.tensor_tensor(out=smc[:, WEND:WEND + 16], in0=smc[:, AEND:AEND + 16],
                                                               in1=smc[:, ACOL:ACOL + 16], op=ALU.subtract),
                     reads=[smb[c]], writes=[smb[c]])
                S.op("act", lambda e, smc=smc: e.activation(out=smc[:, WEND:WEND + 16], in_=smc[:, WEND:WEND + 16], func=AF.Exp),
                     reads=[smb[c]], writes=[smb[c]])
                S.op("dve", lambda e, smc=smc: e.tensor_tensor(out=smc[:, WEND:WEND + 16], in0=smc[:, WEND:WEND + 16],
                                                               in1=smc[:, DT0:DT0 + 16], op=ALU.mult),
                     reads=[smb[c]], writes=[smb[c]])
                S.op("act", lambda e, smc=smc: e.activation(out=smc[:, EEND0:EEND0 + 16 * nseq], in_=smc[:, EEND0:EEND0 + 16 * nseq],
                                                            func=AF.Exp),
                     reads=[smb[c]], writes=[smb[c]])
                xt3 = x_tok[:].rearrange("p (j q) -> p j q", q=64)
                if full:
                    def mcb(e, cr=cr):
                        for g in range(2):
                            i = e.matmul(bank(4)[:, 64 + g * 128:64 + (g + 1) * 128], lhsT=XB(8 + g)[:, cr],
                                         rhs=XB(10 + g)[:, cr], start=True, stop=True)
                        return i
                    S.op("pe", mcb, reads=xbcTb[8:12], writes=[bb[4]])
                    S.op("dve", lambda e: e.tensor_tensor(
                        out=cbm[:], in0=bank(4)[:, 64:320].rearrange("p (g t) -> p g t", t=128),
                        in1=Um.unsqueeze(1).broadcast_to([128, 2, 128]), op=ALU.mult),
                        reads=[bb[4], b_msk], writes=[b_cbm])
                    S.op("dve", lambda e, smc=smc: e.tensor_tensor(
                        out=Rt[:], in0=smc[:, DTA0:DTA0 + 16].unsqueeze(2).broadcast_to([128, 16, 128]),
                        in1=Um.unsqueeze(1).broadcast_to([128, 16, 128]), op=ALU.mult),
                        reads=[smb[c], b_msk], writes=[b_R])
                    Rf = Rt[:].rearrange("p j t -> p (j t)")
                    for qb in range(4):
                        S.op("pe", lambda e, qb=qb: e.matmul(bank(qb), lhsT=E_p, rhs=Rf[:, qb * 512:(qb + 1) * 512],
                                                             start=True, stop=True),
                             reads=[b_msk, b_R], writes=[bb[qb]])
                    for qb in range(4):
                        S.op("dve", lambda e, qb=qb, smc=smc: e.tensor_tensor(
                            out=Rt[:, 4 * qb:4 * qb + 4, :], in0=bank(qb).rearrange("p (j t) -> p j t", t=128),
                            in1=smc[:, ACOL + 4 * qb:ACOL + 4 * qb + 4].unsqueeze(2).broadcast_to([128, 4, 128]),
                            op=ALU.min), reads=[bb[qb], smb[c]], writes=[b_R])
                    S.op("dve", lambda e, smc=smc: e.tensor_tensor(
                        out=Rt[:], in0=Rt[:], in1=smc[:, ACOL:ACOL + 16].unsqueeze(2).broadcast_to([128, 16, 128]),
                        op=ALU.subtract), reads=[b_R, smb[c]], writes=[b_R])
                    S.op("act", lambda e: e.activation(out=Rf, in_=Rf, func=AF.Exp), reads=[b_R], writes=[b_R])
                    S.op("dve", lambda e, smc=smc: e.tensor_tensor(
                        out=Rt[:], in0=Rt[:], in1=smc[:, DT0:DT0 + 16].unsqueeze(2).broadcast_to([128, 16, 128]),
                        op=ALU.mult), reads=[b_R, smb[c]], writes=[b_R])
                    S.op("dve", lambda e: e.tensor_tensor(
                        out=wT[:].rearrange("p (g j) t -> p g j t", g=2), in0=Rt[:].rearrange("p (g j) t -> p g j t", g=2),
                        in1=cbm[:].unsqueeze(2).broadcast_to([128, 2, 8, 128]), op=ALU.mult),
                        reads=[b_R, b_cbm], writes=[wTb[0]])
                    def mmy(e):
                        for j in range(16):
                            i = e.matmul(PB[:, j * 64:(j + 1) * 64], lhsT=wT[:, j, :], rhs=x_tok[:, j * 64:(j + 1) * 64],
                                         start=True, stop=True)
                        return i
                    S.op("pe", mmy, reads=[wTb[0], b_xtok], writes=[bb[0], bb[1]])
                    if smp:
                        S.op("dve", lambda e: e.memset(CTm[:], 0.0), writes=[b_CTm])
                        for q in range(2):
                            for g in range(2):
                                S.op("dve", lambda e, q=q, g=g, c0=c * 128 + q * 64: e.tensor_copy(
                                    out=CTm[:, q, g, q * 64:(q + 1) * 64], in_=XB(10 + g)[:, c0:c0 + 64]),
                                    reads=[xbcTb[10 + g]], writes=[b_CTm])

                    def mmys(e, cr=cr):
                        for g in range(2):
                            for q in range(nseq):
                                lt = CTm[:, q, g, :] if smp else XB(10 + g)[:, cr]
                                i = e.matmul(PB[:, 1024 + g * 512:1024 + (g + 1) * 512], lhsT=lt,
                                             rhs=Sb[q][:, g * 512:(g + 1) * 512], start=(q == 0), stop=(q == nseq - 1))
                        return i
                    S.op("pe", mmys, reads=xbcTb[10:12] + [b_CTm] + Sbb[0:nseq], writes=[bb[2], bb[3]])
                    S.op("act", lambda e, smc=smc: e.activation(out=smc[:, EAC:EAC + 16], in_=smc[:, ACOL:ACOL + 16], func=AF.Exp),
                         reads=[smb[c]], writes=[smb[c]])
                    yb3 = ybuf[:].rearrange("p (j q) -> p j q", q=64)
                    for g in range(2):
                        S.op("dve", lambda e, g=g, smc=smc: e.tensor_tensor(
                            out=yb3[:, 8 * g:8 * g + 8, :], in0=bank(2 + g).rearrange("p (j q) -> p j q", q=64),
                            in1=smc[:, EAC + 8 * g:EAC + 8 * g + 8].unsqueeze(2).broadcast_to([128, 8, 64]), op=ALU.mult),
                            reads=[bb[2 + g], smb[c]], writes=[b_ybuf])
                    for g in range(2):
                        S.op("dve", lambda e, g=g: e.tensor_tensor(out=ybuf[:, g * 512:(g + 1) * 512],
                                                                   in0=ybuf[:, g * 512:(g + 1) * 512], in1=bank(g), op=ALU.add),
                             reads=[bb[g], b_ybuf], writes=[b_ybuf])
                    S.op("dve", lambda e: e.tensor_tensor(
                        out=xd[:].rearrange("p (j q) -> p j q", q=64), in0=xt3,
                        in1=C("dsk").unsqueeze(2).broadcast_to([128, 16, 64]), op=ALU.mult),
                        reads=[b_xtok, b_cst], writes=[b_xd])
                    S.op("dve", lambda e: e.tensor_tensor(out=ybuf[:], in0=ybuf[:], in1=xd[:], op=ALU.add),
                         reads=[b_ybuf, b_xd], writes=[b_ybuf])
                    S.op("dve", lambda e, c=c: e.tensor_tensor(out=ybuf[:], in0=ybuf[:], in1=sz[:, c, :], op=ALU.mult),
                         reads=[b_ybuf, szb[c]], writes=[b_ybuf])
                    rstd_of(ybuf[:], [b_ybuf], c)
                    S.op("act", lambda e, c=c: e.activation(out=xsbf[:, c, :], in_=ybuf[:], func=AF.Identity,
                                                            scale=nst[:, c, RSTD:RSTD + 1]),
                         reads=[b_ybuf, nsb[c]], writes=[xsb[c]])
                    transpose_scale(xsbf[:, c, :], xsb[c], yTall[:, :, cr], yTb[c], "g_ssd", 0)

                S.op("dve", lambda e, smc=smc: e.tensor_tensor(
                    out=xw[:].rearrange("p (j q) -> p j q", q=64), in0=xt3,
                    in1=smc[:, WEND:WEND + 16].unsqueeze(2).broadcast_to([128, 16, 64]), op=ALU.mult),
                    reads=[b_xtok, smb[c]], writes=[b_xw])
                for q in range(nseq):
                    rows = slice(q * 64, (q + 1) * 64) if smp else slice(0, 128)

                    def mms(e, rows=rows):
                        for g in range(2):
                            i = e.matmul(bank(4 + g), lhsT=B_tok[rows, g * 128:(g + 1) * 128],
                                         rhs=xw[rows, g * 512:(g + 1) * 512], start=True, stop=True)
                        return i
                    S.op("pe", mms, reads=[b_Btok, b_xw], writes=[bb[4], bb[5]])
                    s3 = St[q][:].rearrange("p (j q) -> p j q", q=64)
                    S.op("dve", lambda e, s3=s3, smc=smc, q=q: e.tensor_tensor(
                        out=s3, in0=s3, in1=smc[:, EEND0 + 16 * q:EEND0 + 16 * q + 16].unsqueeze(2).broadcast_to([128, 16, 64]),
                        op=ALU.mult), reads=[Stb[q], smb[c]], writes=[Stb[q]])
                    for g in range(2):
                        S.op("dve", lambda e, q=q, g=g: e.tensor_tensor(out=St[q][:, g * 512:(g + 1) * 512],
                                                                        in0=St[q][:, g * 512:(g + 1) * 512], in1=bank(4 + g), op=ALU.add),
                             reads=[Stb[q], bb[4 + g]], writes=[Stb[q]])
                    if fcol is not None and c == nch - 1:
                        S.op("dve", lambda e, q=q: e.tensor_scalar(out=St[q][:], in0=St[q][:], scalar1=fcol, scalar2=None,
                                                                   op0=ALU.mult),
                             reads=[Stb[q], b_flg], writes=[Stb[q]])
                    S.op("act", lambda e, q=q: e.activation(out=Sb[q][:], in_=St[q][:], func=AF.Copy),
                         reads=[Stb[q]], writes=[Sbb[q]])

            if full:
                for qd in range(4):
                    wd_load(qd % 2, W["w_out"][:, qd * 256:(qd + 1) * 256], 16)
                    for c in range(nch):
                        cr = slice(c * 128, (c + 1) * 128)
                        po, pob = bank(c % 2), bb[c % 2]

                        def mmo(e, cr=cr, qd=qd, po=po):
                            for mc in range(8):
                                e.matmul(po[:, 0:256], lhsT=cT[:, mc, cr], rhs=WDq[qd % 2][:, mc, :], start=(mc == 0), stop=False)
                            for mc in range(8):
                                i = e.matmul(po[:, 0:256], lhsT=yTall[:, mc, cr], rhs=WDq[qd % 2][:, 8 + mc, :], start=False,
                                             stop=(mc == 7))
                            return i
                        S.op("pe", mmo, reads=cTb + [yTb[c]] + wdb[qd % 2], writes=[pob])
                        hs = h[:, c, qd * 256:(qd + 1) * 256]
                        S.op("dve", lambda e, po=po, hs=hs: e.tensor_tensor(out=hs, in0=hs, in1=po[:, 0:256], op=ALU.add),
                             reads=[pob, hb[c]], writes=[hb[c]])

        def ssd_prefix(nch, XB, xbcTb, fcol, hook_unused=None):
            xt4 = yTall[:].rearrange("p a t -> p (a t)").rearrange("p (c d) -> p c d", d=D)
            bt4 = wT[:].rearrange("p j t -> p (j t)")[:, 0:4 * 256].rearrange("p (c d) -> p c d", d=256)
            st = {}
            pac = bank(5)
            TAIL = NACOL

            def p_mac():
                _b = RYW.phase([f"xtk{c}" for c in range(4)] + [f"btk{c}" for c in range(4)])
                st["xtb"], st["btb"] = _b[0:4], _b[4:8]
                for c in range(nch):
                    smc = sm[:, c, :]
                    S.op("pe", lambda e, smc=smc, c=c: (e.matmul(pac[:, c * 64:c * 64 + 16], lhsT=U_p, rhs=smc[:, DTA0:DTA0 + 16],
                                                                 start=True, stop=True),
                                                        e.matmul(pac[:, c * 64 + 16:c * 64 + 32], lhsT=E_p, rhs=smc[:, DTA0:DTA0 + 16],
                                                                 start=True, stop=True))[1],
                         reads=[b_msk, smb[c]], writes=[bb[5]])
                for c in range(nch):
                    smc = sm[:, c, :]
                    S.op("dve", lambda e, smc=smc, c=c: e.tensor_copy(out=smc[:, ACOL:ACOL + 32], in_=pac[:, c * 64:c * 64 + 32]),
                         reads=[bb[5]], writes=[smb[c]])
                S.op("dve", lambda e: e.memset(sm[:, nch - 1, TAIL:TAIL + 16], 0.0), writes=[smb[nch - 1]])
                for c in range(nch - 2, -1, -1):
                    S.op("dve", lambda e, c=c: e.tensor_tensor(out=sm[:, c, TAIL:TAIL + 16], in0=sm[:, c + 1, TAIL:TAIL + 16],
                                                               in1=sm[:, c + 1, AEND:AEND + 16], op=ALU.add),
                         reads=[smb[c + 1]], writes=[smb[c]])
                S.op("dve", lambda e: e.tensor_tensor(out=sm[:, 0, EEND0:EEND0 + 16], in0=sm[:, 0, TAIL:TAIL + 16],
                                                      in1=sm[:, 0, AEND:AEND + 16], op=ALU.add),
                     reads=[smb[0]], writes=[smb[0]])
                for c in range(nch):
                    smc = sm[:, c, :]
                    S.op("dve", lambda e, smc=smc: e.tensor_tensor(out=smc[:, WEND:WEND + 16], in0=smc[:, AEND:AEND + 16],
                                                                   in1=smc[:, ACOL:ACOL + 16], op=ALU.subtract),
                         reads=[smb[c]], writes=[smb[c]])
                for c in range(nch):
                    smc = sm[:, c, :]
                    S.op("dve", lambda e, smc=smc: e.tensor_tensor(out=smc[:, WEND:WEND + 16], in0=smc[:, WEND:WEND + 16],
                                                                   in1=smc[:, TAIL:TAIL + 16], op=ALU.add),
                         reads=[smb[c]], writes=[smb[c]])
                for c in range(nch):
                    smc = sm[:, c, :]
                    S.op("act", lambda e, smc=smc: e.activation(out=smc[:, WEND:WEND + 16], in_=smc[:, WEND:WEND + 16], func=AF.Exp),
                         reads=[smb[c]], writes=[smb[c]])
                S.op("act", lambda e: e.activation(out=sm[:, 0, EEND0:EEND0 + 16], in_=sm[:, 0, EEND0:EEND0 + 16], func=AF.Exp),
                     reads=[smb[0]], writes=[smb[0]])
                for c in range(nch):
                    smc = sm[:, c, :]
                    S.op("dve", lambda e, smc=smc: e.tensor_tensor(out=smc[:, WEND:WEND + 16], in0=smc[:, WEND:WEND + 16],
                                                                   in1=smc[:, DT0:DT0 + 16], op=ALU.mult),
                         reads=[smb[c]], writes=[smb[c]])

            def p_tr(c):
                cr = slice(c * 128, (c + 1) * 128)

                def trx(e, cr=cr):
                    for j in range(8):
                        i = e.transpose(out=PTb[0][:, j, :], in_=XB(j)[:, cr], identity=ident_b[:])
                    return i
                S.op("pe", trx, reads=xbcTb[0:8] + [b_idb], writes=[ptb[0]])
                S.op("act", lambda e, c=c: e.activation(out=xt4[:, c, :], in_=PTb[0][:].rearrange("p a b -> p (a b)"), func=AF.Copy),
                     reads=[ptb[0]], writes=[st["xtb"][c]])

                def trb(e, cr=cr):
                    for g in range(2):
                        i = e.transpose(out=PTb[1][:, g, :], in_=XB(8 + g)[:, cr], identity=ident_b[:])
                    return i
                S.op("pe", trb, reads=xbcTb[8:10] + [b_idb], writes=[ptb[1]])
                S.op("act", lambda e, c=c: e.activation(out=bt4[:, c, :], in_=PTb[1][:, 0:2, :].rearrange("p a b -> p (a b)"),
                                                        func=AF.Copy),
                     reads=[ptb[1]], writes=[st["btb"][c]])

            def p_xw(c):
                xwc, xwcb = xw2[:, c % 2, :], xwb[c % 2]
                S.op("dve", lambda e, c=c, xwc=xwc: e.tensor_tensor(
                    out=xwc.rearrange("p (j q) -> p j q", q=64), in0=xt4[:, c, :].rearrange("p (j q) -> p j q", q=64),
                    in1=sm[:, c, WEND:WEND + 16].unsqueeze(2).broadcast_to([128, 16, 64]), op=ALU.mult),
                    reads=[st["xtb"][c], smb[c]], writes=[xwcb])

            def p_mm(c):
                xwc, xwcb = xw2[:, c % 2, :], xwb[c % 2]

                def mms(e, c=c, xwc=xwc):
                    for g in range(2):
                        i = e.matmul(bank(4 + g), lhsT=bt4[:, c, g * 128:(g + 1) * 128], rhs=xwc[:, g * 512:(g + 1) * 512],
                                     start=(c == 0), stop=(c == nch - 1))
                    return i
                S.op("pe", mms, reads=[st["btb"][c], xwcb], writes=[bb[4], bb[5]])

            def p_fin():
                s3 = St[0][:].rearrange("p (j q) -> p j q", q=64)
                S.op("dve", lambda e: e.tensor_tensor(out=s3, in0=s3,
                                                      in1=sm[:, 0, EEND0:EEND0 + 16].unsqueeze(2).broadcast_to([128, 16, 64]),
                                                      op=ALU.mult), reads=[Stb[0], smb[0]], writes=[Stb[0]])
                for g in range(2):
                    S.op("dve", lambda e, g=g: e.tensor_tensor(out=St[0][:, g * 512:(g + 1) * 512], in0=St[0][:, g * 512:(g + 1) * 512],
                                                               in1=bank(4 + g), op=ALU.add),
                         reads=[Stb[0], bb[4 + g]], writes=[Stb[0]])
                if fcol is not None:
                    S.op("dve", lambda e: e.tensor_scalar(out=St[0][:], in0=St[0][:], scalar1=fcol, scalar2=None, op0=ALU.mult),
                         reads=[Stb[0], b_flg], writes=[Stb[0]])
                S.op("act", lambda e: e.activation(out=Sb[0][:], in_=St[0][:], func=AF.Copy), reads=[Stb[0]], writes=[Sbb[0]])

            seq = lambda *fs: (lambda: [f() for f in fs])
            return {
                1: p_mac,
                2: lambda: p_tr(0), 3: lambda: p_tr(1), 4: lambda: p_tr(2), 5: lambda: p_tr(3),
                6: seq(lambda: p_xw(0), lambda: p_xw(1)),
                7: seq(lambda: p_mm(0), lambda: p_xw(2)),
                8: seq(lambda: p_mm(1), lambda: p_xw(3)),
                9: lambda: p_mm(2),
                10: seq(lambda: p_mm(3), p_fin),
            }

        def load_hist_T(src, nrow, nchunk, dst, col0, dstb):
            S.dma("sp", lambda e: e.dma_start(out=stg[0:nrow, 0:nchunk * 128], in_=src), writes=[b_stg, b_ybuf, b_xd])
            flat = stg
            for ch in range(nchunk):
                pt = bank(4 + ch % 2)
                S.op("pe", lambda e, ch=ch, pt=pt: e.transpose(out=pt[:, 0:nrow], in_=flat[0:nrow, ch * 128:(ch + 1) * 128],
                                                               identity=ident_f[0:nrow, 0:nrow]),
                     reads=[b_stg, b_ybuf, b_xd, b_idf], writes=[bb[4 + ch % 2]])
                S.op("act", lambda e, ch=ch, pt=pt: e.activation(out=dst[:, ch, col0:col0 + nrow], in_=pt[:, 0:nrow], func=AF.Copy),
                     reads=[bb[4 + ch % 2]], writes=[dstb[ch]])

        def ple_final(nch, p_src, y_dst):
            norms(nch, "g_ple")
            for c in range(nch):
                S.dma("sp", lambda e, c=c: e.dma_start(out=pbuf[:], in_=p_src[c * 128:(c + 1) * 128, :]), writes=[b_pbuf])
                S.op("dve", lambda e: e.tensor_copy(out=pb16[:], in_=pbuf[:]), reads=[b_pbuf], writes=[b_pb16])

                def trp(e):
                    for k in range(2):
                        i = e.transpose(out=PTb[1][:, k, :], in_=pb16[:, k * 128:(k + 1) * 128], identity=ident_b[:])
                    return i
                S.op("pe", trp, reads=[b_pb16, b_idb], writes=[ptb[1]])
                S.op("act", lambda e, c=c: e.activation(out=pTall[:, :, c * 128:(c + 1) * 128], in_=PTb[1][:, 0:2, :], func=AF.Copy),
                     reads=[ptb[1]], writes=[pTb[c]])
            for qd in range(4):
                wd_load(qd % 2, W["ple_w_gate"][:, qd * 256:(qd + 1) * 256], 8)
                wd_load(qd % 2, W["ple_w_proj"][:, qd * 256:(qd + 1) * 256], 2, kofs=8)
                for c in range(nch):
                    pg, pp = bank(c % 2), bank(2 + c % 2)

                    def mmg(e, c=c, qd=qd, pg=pg):
                        for k in range(8):
                            i = e.matmul(pg[:, 0:256], lhsT=xnT[:, k, c * 128:(c + 1) * 128], rhs=WDq[qd % 2][:, k, :],
                                         start=(k == 0), stop=(k == 7))
                        return i
                    S.op("pe", mmg, reads=[xnTb[c]] + wdb[qd % 2], writes=[bb[c % 2]])

                    def mmp(e, c=c, qd=qd, pp=pp):
                        for k in range(2):
                            i = e.matmul(pp[:, 0:256], lhsT=pTall[:, k, c * 128:(c + 1) * 128], rhs=WDq[qd % 2][:, 8 + k, :],
                                         start=(k == 0), stop=(k == 1))
                        return i
                    S.op("pe", mmp, reads=[pTb[c]] + wdb[qd % 2], writes=[bb[2 + c % 2]])
                    S.op("act", lambda e, c=c, pg=pg: e.activation(out=sg[:, c % 2, 0:256], in_=pg[:, 0:256], func=AF.Sigmoid),
                         reads=[bb[c % 2]], writes=[sgb[c % 2]])
                    S.op("dve", lambda e, c=c, pp=pp: e.tensor_tensor(out=sg[:, c % 2, 0:256], in0=sg[:, c % 2, 0:256],
                                                                      in1=pp[:, 0:256], op=ALU.mult),
                         reads=[sgb[c % 2], bb[2 + c % 2]], writes=[sgb[c % 2]])
                    hs = h[:, c, qd * 256:(qd + 1) * 256]
                    S.op("dve", lambda e, c=c, hs=hs: e.tensor_tensor(out=hs, in0=hs, in1=sg[:, c % 2, 0:256], op=ALU.add),
                         reads=[sgb[c % 2], hb[c]], writes=[hb[c]])
            cs = list(range(nch))
            rstd_stage([(h[:, c, :], [hb[c]]) for c in cs], cs)
            for c in cs:
                ob, obb = (xd, b_xd) if c % 2 == 0 else (ybuf, b_ybuf)
                S.op("dve", lambda e, c=c, ob=ob: e.scalar_tensor_tensor(out=ob[:], in0=h[:, c, :], scalar=nst[:, c, RSTD:RSTD + 1],
                                                                         in1=C("fin"), op0=ALU.mult, op1=ALU.mult),
                     reads=[hb[c], nsb[c], b_cst], writes=[obb])
                out_dmas.append(S.dma("sp", lambda e, c=c, ob=ob: e.dma_start(out=y_dst[c * 128:(c + 1) * 128, :], in_=ob[:]),
                                      reads=[obb]))

        def emit_state_outputs(nseq, o_conv, o_xbc, o_ssd):
            flat = stg
            stg3 = stg[:, 0:1024].rearrange("p (a b) -> p a b", b=128)
            for q in range(nseq):
                oc = o_conv[q] if nseq > 1 else o_conv
                ox = o_xbc[q] if nseq > 1 else o_xbc
                osd = o_ssd[q] if nseq > 1 else o_ssd
                for ch in range(8):
                    pt = bank(ch % 2)
                    S.op("pe", lambda e, ch=ch, pt=pt, q=q: e.transpose(out=pt[0:32, 0:128], in_=a_last[:, q, ch, :],
                                                                        identity=ident_f[:]),
                         reads=[b_alast, b_idf], writes=[bb[ch % 2]])
                    S.op("act", lambda e, ch=ch, pt=pt: e.activation(out=flat[0:32, ch * 128:(ch + 1) * 128], in_=pt[0:32, 0:128],
                                                                     func=AF.Copy),
                         reads=[bb[ch % 2]], writes=[b_stg, b_ybuf, b_xd])
                out_dmas.append(S.dma("sp", lambda e, oc=oc: e.dma_start(out=oc, in_=flat[2:32, 0:1024]), reads=[b_stg, b_ybuf, b_xd]))
                for xc in range(12):
                    pt = bank(xc % 2)
                    S.op("pe", lambda e, xc=xc, pt=pt, q=q: e.transpose(out=pt[0:4, 0:128], in_=x_last[:, q, xc, :],
                                                                        identity=ident_f[:]),
                         reads=[b_xlast, b_idf], writes=[bb[xc % 2]])
                    S.op("act", lambda e, xc=xc, pt=pt: e.activation(out=flat[0:4, xc * 128:(xc + 1) * 128], in_=pt[0:4, 0:128],
                                                                     func=AF.Copy),
                         reads=[bb[xc % 2]], writes=[b_stg, b_ybuf, b_xd])
                out_dmas.append(S.dma("sp", lambda e, ox=ox: e.dma_start(out=ox, in_=flat[1:4, 0:1536]), reads=[b_stg, b_ybuf, b_xd]))
                for kc in range(8):
                    pt = bank(kc % 2)
                    S.op("pe", lambda e, kc=kc, pt=pt, q=q: e.transpose(out=pt[:, 0:128], in_=St[q][:, kc * 128:(kc + 1) * 128],
                                                                        identity=ident_f[:]),
                         reads=[Stb[q], b_idf], writes=[bb[kc % 2]])
                    S.op("act", lambda e, kc=kc, pt=pt: e.activation(out=stg3[:, kc, :], in_=pt[:, 0:128], func=AF.Copy),
                         reads=[bb[kc % 2]], writes=[b_stg, b_ybuf, b_xd])
                out_dmas.append(S.dma("sp", lambda e, osd=osd: e.dma_start(out=osd.rearrange("(kc p) n -> p kc n", p=128),
                                                                           in_=stg3), reads=[b_stg, b_ybuf, b_xd]))

        def load_x(nch, src):
            for c in range(nch):
                S.dma("sp", lambda e, c=c: e.dma_start(out=h[:, c, :], in_=src[c * 128:(c + 1) * 128, :]), writes=[hb[c]])

        if NT > 0:
            S.op("dve", lambda e: e.memset(St[0][:], 0.0), writes=[Stb[0]])
            S.op("dve", lambda e: e.memset(Sb[0][:], 0.0), writes=[Sbb[0]])
        defer = [None]
        for t in range(NT):
            mode = "full" if t >= NPRE else ("prefix_last" if t == NPRE - 1 else "prefix")
            fcol = flg[:, t:t + 1] if t < NPRE else None
            if t == 0:
                load_x(4, xin[0:TT, :])
                norms(4, "g_ffn1", "B")
            hk = defer[0]
            ffn(4, W["ffn1_w_gate"], W["ffn1_w_up"], W["ffn1_w_down"], "g_ffn1", do_norm=False, hooks=hk, sel="B")
            defer[0] = None
            nxt = (lambda tn=t + 1: load_x(4, xin[tn * TT:(tn + 1) * TT, :])) if t + 1 < NT else None
            nrm = (lambda part=0: norms(4, "g_ffn1", "B", part)) if t + 1 < NT else None
            if mode == "full":
                if t == NPRE:
                    b_ybuf, b_xd = RYX.phase(["ybuf", "xd"])
                mixer(4, [(0, TT)], mode, fcol, t == 0, 1, False, need_last=(t == NT - 1))
                ffn(4, W["ffn2_w_gate"], W["ffn2_w_up"], W["ffn2_w_down"], "g_ffn2")
                tf = t - NPRE
                ple_final(4, pin[tf * TT:(tf + 1) * TT, :], yp[tf * TT:(tf + 1) * TT, :])
                if nxt is not None:
                    nxt()
                    nrm()
            else:
                defer[0] = mixer(4, [(0, TT)], mode, fcol, t == 0, 1, False, hook_after_norm=nxt, hook_after_ssd1=nrm, need_last=False)
        if NFULL > 0:
            emit_state_outputs(1, o_conv_p, o_xbc_p, o_ssd_p)
        if SAMPLE:
            for q in range(2):
                S.dma("sp", lambda e, q=q: e.dma_start(out=stg[:, 0:1024].rearrange("p (a b) -> p a b", b=128),
                                                       in_=sssd[q].rearrange("(kc p) n -> p kc n", p=128)),
                      writes=[b_stg, b_ybuf, b_xd])
                for kc in range(8):
                    pt = bank(kc % 2)
                    S.op("pe", lambda e, kc=kc, pt=pt: e.transpose(out=pt[:, 0:128], in_=stg[:, kc * 128:(kc + 1) * 128], identity=ident_f[:]),
                         reads=[b_stg, b_ybuf, b_xd, b_idf], writes=[bb[kc % 2]])
                    S.op("act", lambda e, kc=kc, pt=pt, q=q: e.activation(out=St[q][:, kc * 128:(kc + 1) * 128], in_=pt[:, 0:128],
                                                                          func=AF.Copy),
                         reads=[bb[kc % 2]], writes=[Stb[q]])
                S.op("act", lambda e, q=q: e.activation(out=Sb[q][:], in_=St[q][:], func=AF.Copy), reads=[Stb[q]], writes=[Sbb[q]])
            load_x(1, xs_in)
            ffn(1, W["ffn1_w_gate"], W["ffn1_w_up"], W["ffn1_w_down"], "g_ffn1")
            mixer(1, [(0, 64), (64, 64)], "full", None, True, 2, True)
            ffn(1, W["ffn2_w_gate"], W["ffn2_w_up"], W["ffn2_w_down"], "g_ffn2")
            ple_final(1, ps_in, ys)
            emit_state_outputs(2, o_conv_s, o_xbc_s, o_ssd_s)
        S.emit(final_waits=out_dmas)
    return nc


NPRE_FULL, NFULL_FULL = 24, 8
_prog_cache = {}


def _get_prog(npre, nfull, sample):
    key = (npre, nfull, sample)
    if key not in _prog_cache:
        _prog_cache[key] = build_program(npre, nfull, sample)
    return _prog_cache[key]


def make_core_inputs(inp, seg_per_seq, npre, nfull):
    B = inp["x_prompt"].shape[0]
    ncores = B * seg_per_seq
    consts = pack_consts(inp)
    masks = make_masks()
    seglen = nfull * TT
    maps = []
    for core in range(ncores):
        b, k = divmod(core, seg_per_seq)
        xin = np.zeros(((npre + nfull) * TT, D), np.float32)
        flags = np.zeros((128, npre + nfull), np.float32)
        npref = k * seglen
        if npref > 0:
            xin[npre * TT - npref:npre * TT] = inp["x_prompt"][b, 0:npref]
            flags[:, npre - npref // TT:npre] = 1.0
        xin[npre * TT:] = inp["x_prompt"][b, k * seglen:(k + 1) * seglen]
        pin = np.ascontiguousarray(inp["p_prompt"][0, b, k * seglen:(k + 1) * seglen])
        m = {"xin": xin, "pin": pin, "flags": flags, "consts": consts, "masks": masks}
        s0 = 2 * core
        m["xs_in"] = np.ascontiguousarray(inp["x_sample"][s0:s0 + 2].reshape(128, D))
        m["ps_in"] = np.ascontiguousarray(inp["p_sample"][0, s0:s0 + 2].reshape(128, 256))
        m["sconv"] = np.ascontiguousarray(inp["state_conv"][0, s0:s0 + 2])
        m["sxbc"] = np.ascontiguousarray(inp["state_ssd_conv"][0, s0:s0 + 2])
        m["sssd"] = np.ascontiguousarray(inp["state_ssd"][0, s0:s0 + 2].reshape(2, 1024, 128))
        for n in WNAMES:
            m[n] = np.ascontiguousarray(inp[n][0])
        maps.append(m)
    return maps


def assemble(res, inp, seg_per_seq, nfull):
    B = inp["x_prompt"].shape[0]
    ncores = B * seg_per_seq
    seglen = nfull * TT
    yp = np.zeros((B, seg_per_seq * seglen, D), np.float32)
    ys = np.zeros((2 * ncores, 64, D), np.float32)
    conv_p = np.zeros((1, B, 30, 1024), np.float32)
    xbc_p = np.zeros((1, B, 3, 1536), np.float32)
    ssd_p = np.zeros((1, B, 16, 64, 128), np.float32)
    conv_s = np.zeros((1, 2 * ncores, 30, 1024), np.float32)
    xbc_s = np.zeros((1, 2 * ncores, 3, 1536), np.float32)
    ssd_s = np.zeros((1, 2 * ncores, 16, 64, 128), np.float32)
    for core in range(ncores):
        r = res[core]
        b, k = divmod(core, seg_per_seq)
        yp[b, k * seglen:(k + 1) * seglen] = r["yp"]
        ys[2 * core:2 * core + 2] = r["ys"].reshape(2, 64, D)
        if k == seg_per_seq - 1:
            conv_p[0, b] = r["o_conv_p"]
            xbc_p[0, b] = r["o_xbc_p"]
            ssd_p[0, b] = r["o_ssd_p"].reshape(16, 64, 128)
        conv_s[0, 2 * core:2 * core + 2] = r["o_conv_s"]
        xbc_s[0, 2 * core:2 * core + 2] = r["o_xbc_s"]
        ssd_s[0, 2 * core:2 * core + 2] = r["o_ssd_s"].reshape(2, 16, 64, 128)
    return yp, ys, conv_p, xbc_p, ssd_p, conv_s, xbc_s, ssd_s


def kernel(**inputs):
    inp = {k: np.asarray(v) for k, v in inputs.items()}
    seg = 4
    nfull = inp["x_prompt"].shape[1] // (seg * TT)
    npre = (seg - 1) * nfull
    nc = _get_prog(npre, nfull, True)
    maps = make_core_inputs(inp, seg, npre, nfull)
    res = run_bass_kernel_spmd(nc, maps, core_ids=list(range(len(maps))))
    return assemble(res.results, inp, seg, nfull)
```

```python
import contextlib
import numpy as np
import concourse.bass as bass
import concourse.mybir as mybir
from concourse.bass_utils import run_bass_kernel_spmd

F32 = mybir.dt.float32
BF16 = mybir.dt.bfloat16
AF = mybir.ActivationFunctionType
ALU = mybir.AluOpType

D = 1024
DFF = 2816
NFC = DFF // 128
INP = 4624
TT = 512
import os as _os
PEND_MAX = int(_os.environ.get('PEND_MAX', '2'))
STATS_DELAY = int(_os.environ.get('STATS_DELAY', '1'))
EPS = 1e-6


class Buf:
    __slots__ = ("name", "last_w", "readers")

    def __init__(self, name="", fence=()):
        self.name = name
        self.last_w = None
        self.readers = list(fence)


class Op:
    __slots__ = ("eng", "idx", "fn", "deps", "is_target", "semval", "dma_sem", "dma_val", "dma_prev")

    def __init__(self, eng, idx, fn, deps):
        self.eng, self.idx, self.fn, self.deps = eng, idx, fn, deps
        self.is_target = False
        self.semval = None
        self.dma_sem = None
        self.dma_val = None
        self.dma_prev = None


class EngQ:
    def __init__(self, name):
        self.name = name
        self.ops = []
        self.sem = None


class Sched:
    ENG_NAMES = ("pe", "act", "dve", "pool", "sp")

    def __init__(self, nc, dma_ring=24):
        self.nc = nc
        self.q = {n: EngQ(n) for n in self.ENG_NAMES}
        self.dma_ring = dma_ring
        self.dma_sems = []
        self.dma_count = 0
        self.dma_cnt_q = {}
        self.dma_ops = []

    def capture_begin(self):
        self._cap = []

    def capture_end(self):
        c, self._cap = self._cap, None
        return c

    def mark(self):
        self._cap.append(None)

    def replay(self, item):
        eng, fn, reads, writes = item
        return self.op(eng, fn, reads, writes)

    def op(self, eng, fn, reads=(), writes=(), extra=(), dma=False):
        if getattr(self, "_cap", None) is not None and not dma:
            self._cap.append((eng, fn, list(reads), list(writes)))
            return None
        q = self.q[eng]
        deps = []
        for b in reads:
            if b.last_w is not None:
                deps.append(b.last_w)
        for b in writes:
            if b.last_w is not None:
                deps.append(b.last_w)
            deps.extend(b.readers)
        deps.extend(extra)
        o = Op(q, len(q.ops), fn, deps)
        q.ops.append(o)
        for b in writes:
            b.last_w = o
            b.readers = []
        for b in reads:
            if b.last_w is not o:
                b.readers.append(o)
        return o

    def dma(self, eng, fn, reads=(), writes=(), extra=()):
        o = self.op(eng, fn, reads, writes, extra, dma=True)
        half = self.dma_ring // 2
        cnt = self.dma_cnt_q.setdefault(eng, 0)
        self.dma_cnt_q[eng] = cnt + 1
        o.dma_sem = (cnt % half) + (half if eng == "pool" else 0)
        self.dma_count += 1
        self.dma_ops.append(o)
        return o

    def finalize(self):
        for q in self.q.values():
            for o in q.ops:
                for d in o.deps:
                    d.is_target = True
        last = [None] * self.dma_ring
        cnt = [0] * self.dma_ring
        for o in self.dma_ops:
            s = o.dma_sem
            o.dma_prev = last[s]
            cnt[s] += 16
            o.dma_val = cnt[s]
            last[s] = o
        for q in self.q.values():
            v = 0
            for o in q.ops:
                if o.dma_sem is None and o.is_target:
                    v += 1
                    o.semval = v

    def emit(self, final_waits=()):
        nc = self.nc
        self.finalize()
        with contextlib.ExitStack() as es:
            for q in self.q.values():
                q.sem = es.enter_context(nc.semaphore("s_" + q.name))
            self.dma_sems = [es.enter_context(nc.semaphore(f"d{i}")) for i in range(self.dma_ring)]
            block = es.enter_context(nc.Block())
            engmap = {"pe": block.tensor, "act": block.scalar, "dve": block.vector,
                      "pool": block.gpsimd, "sp": block.sync}
            for name, q in self.q.items():
                if not q.ops and not (name == "sp" and final_waits):
                    continue
                self._emit_q(engmap[name], q, final_waits if name == "sp" else ())

    def _emit_q(self, blockfn, q, final_waits):
        sched = self

        def body(e):
            waited = {}

            def wait(key, sem, val):
                if waited.get(key, 0) >= val:
                    return
                waited[key] = val
                e.wait_ge(sem, val)

            def wait_op(d):
                if d.dma_sem is not None:
                    wait(("d", d.dma_sem), sched.dma_sems[d.dma_sem], d.dma_val)
                else:
                    wait(("e", d.eng.name), d.eng.sem, d.semval)

            for o in q.ops:
                for d in o.deps:
                    wait_op(d)
                if o.dma_sem is not None:
                    if o.dma_prev is not None:
                        wait_op(o.dma_prev)
                    o.fn(e).then_inc(sched.dma_sems[o.dma_sem], 16)
                else:
                    inst = o.fn(e)
                    if o.is_target:
                        inst.then_inc(q.sem, 1)
            for d in final_waits:
                wait_op(d)

        blockfn(body)


def _col8(v):
    return np.ascontiguousarray(v.reshape(-1, 128).T)


class CLayout:
    def __init__(self):
        self.off = {}
        self.n = 0

    def add(self, name, width):
        self.off[name] = (self.n, width)
        self.n += width


CL = CLayout()
for _n in ("g_ffn1", "g_mix", "g_ffn2", "g_ple", "g_ssd", "cb", "lng", "lnb"):
    CL.add(_n, 8)
CL.add("cw", 8 * 31)
CL.add("sw", 12 * 4)
CL.add("sb", 12)
for _n in ("dtb", "alog", "dsk"):
    CL.add(_n, 16)
CL.add("eps", 1)
CL.add("one", 1)
CL.add("fin", 1024)


def pack_consts(inp):
    c = np.zeros((128, CL.n), np.float32)

    def put(name, arr):
        o, w = CL.off[name]
        c[:, o:o + w] = arr.reshape(128, w)
    put("g_ffn1", _col8(inp["ffn1_norm"][0]))
    put("g_mix", _col8(inp["mix_norm"][0]))
    put("g_ffn2", _col8(inp["ffn2_norm"][0]))
    put("g_ple", _col8(inp["ple_norm"][0]))
    put("g_ssd", _col8(inp["ssd_norm"][0]))
    put("cb", _col8(inp["conv_dw_b"][0]))
    put("lng", _col8(inp["conv_ln_g"][0]))
    put("lnb", _col8(inp["conv_ln_b"][0]))
    cw = inp["conv_dw_w"][0]
    put("cw", np.ascontiguousarray(cw.reshape(31, 8, 128).transpose(2, 1, 0)))
    sw = inp["ssd_conv_w"][0]
    put("sw", np.ascontiguousarray(sw.reshape(4, 12, 128).transpose(2, 1, 0)))
    put("sb", np.ascontiguousarray(inp["ssd_conv_b"][0].reshape(12, 128).T))
    put("dtb", np.broadcast_to(inp["ssd_dt_bias"][0][None, :], (128, 16)).copy())
    put("alog", np.broadcast_to(inp["ssd_A_log"][0][None, :], (128, 16)).copy())
    put("dsk", np.broadcast_to(inp["ssd_D"][0][None, :], (128, 16)).copy())
    put("eps", np.full((128, 1), EPS, np.float32))
    put("one", np.full((128, 1), 1.0, np.float32))
    put("fin", np.broadcast_to(inp["final_norm"][None, :], (128, 1024)).copy())
    return c


def make_masks():
    s = np.arange(128)[:, None]
    t = np.arange(128)[None, :]
    same = (s // 64) == (t // 64)
    m = np.zeros((128, 6, 128), np.float32)
    m[:, 0] = (s <= t)
    m[:, 1] = 1.0
    m[:, 2] = (s <= t) & same
    m[:, 3] = same
    m[:, 4] = (s < 64) & (t >= 0)
    m[:, 5] = (s >= 64) & (t >= 0)
    return m


WNAMES = ["ffn1_w_gate", "ffn1_w_up", "ffn1_w_down", "w_in", "w_out",
          "ffn2_w_gate", "ffn2_w_up", "ffn2_w_down", "ple_w_proj", "ple_w_gate"]


def build_program(NPRE, NFULL, SAMPLE=True):
    nc = bass.Bass("TRN2", target_bir_lowering=False)
    NT = NPRE + NFULL
    din = lambda name, shape: nc.dram_tensor(name, shape, F32, kind="ExternalInput").ap()
    dout = lambda name, shape: nc.dram_tensor(name, shape, F32, kind="ExternalOutput").ap()
    xin = din("xin", [max(NT, 1) * TT, D])
    pin = din("pin", [max(NFULL, 1) * TT, 256])
    flags = din("flags", [128, max(NT, 1)])
    xs_in = din("xs_in", [128, D])
    ps_in = din("ps_in", [128, 256])
    sconv = din("sconv", [2, 30, 1024])
    sxbc = din("sxbc", [2, 3, 1536])
    sssd = din("sssd", [2, 1024, 128])
    consts_d = din("consts", [128, CL.n])
    masks_d = din("masks", [128, 6, 128])
    W = {}
    wshape = {"ffn1_w_gate": [D, DFF], "ffn1_w_up": [D, DFF], "ffn1_w_down": [DFF, D], "w_in": [D, INP],
              "w_out": [2048, D], "ffn2_w_gate": [D, DFF], "ffn2_w_up": [D, DFF], "ffn2_w_down": [DFF, D],
              "ple_w_proj": [256, D], "ple_w_gate": [D, D]}
    for n in WNAMES:
        W[n] = din(n, wshape[n])
    yp = dout("yp", [max(NFULL, 1) * TT, D])
    ys = dout("ys", [128, D])
    o_conv_p = dout("o_conv_p", [30, 1024])
    o_xbc_p = dout("o_xbc_p", [3, 1536])
    o_ssd_p = dout("o_ssd_p", [1024, 128])
    o_conv_s = dout("o_conv_s", [2, 30, 1024])
    o_xbc_s = dout("o_xbc_s", [2, 3, 1536])
    o_ssd_s = dout("o_ssd_s", [2, 1024, 128])

    S = Sched(nc)
    es = contextlib.ExitStack()
    sbt = lambda name, shape, dt: es.enter_context(nc.sbuf_tensor(name, shape, dt))
    pst = lambda name, shape, dt: es.enter_context(nc.psum_tensor(name, shape, dt))
    out_dmas = []

    with es:
        cst = sbt("cst", [128, CL.n], F32)
        msk = sbt("msk", [128, 6, 128], F32)
        flg = sbt("flg", [128, max(NT, 1)], F32)
        ident_b = sbt("ident_b", [128, 128], BF16)
        ident_f = sbt("ident_f", [128, 128], F32)
        abc = sbt("abc", [128, 16], F32)
        h = sbt("h", [128, 4, D], F32)
        xsbf = sbt("xsbf", [128, 4, D], BF16)
        nst = sbt("nst", [128, 4, 4], F32)
        xnT = sbt("xnT", [128, 8, TT], BF16)
        regH = sbt("regH", [128, NFC * TT], BF16)
        sg = sbt("sg", [128, 2, TT], F32)
        RS = 5
        ring = [sbt(f"ring{i}", [128, 8, 256], BF16) for i in range(RS)]
        WDq = [sbt(f"wdq{i}", [128, NFC, 256], BF16) for i in range(2)]
        regM = sbt("regM", [128, 12 * (TT + 3) + 4], F32)
        regA = sbt("regA", [128, 8 * (TT + 30)], BF16)
        a_hist = sbt("a_hist", [128, 8, 30], BF16)
        x_hist = sbt("x_hist", [128, 12, 3], BF16)
        a_last = sbt("a_last", [128, 2, 8, 32], F32)
        x_last = sbt("x_last", [128, 2, 12, 4], F32)
        Rt = sbt("Rt", [128, 16, 128], F32)
        cbm = sbt("cbm", [128, 2, 128], F32)
        x_tok = sbt("x_tok", [128, D], BF16)
        B_tok = sbt("B_tok", [128, 256], BF16)
        B_tok2 = sbt("B_tok2", [128, 256], BF16)
        xw2 = sbt("xw2", [128, 2, D], BF16)
        bcT_p = sbt("bcT_p", [128, 4, TT], BF16)
        xw = xw2[:, 0, :]
        yx = sbt("yx", [128, 2 * D], F32)
        ybuf = yx[:, 0:D]
        xd = yx[:, D:2 * D]
        stg = yx[:, 0:1536]
        CTm = sbt("CTm", [128, 2, 2, 128], BF16)
        St = [sbt(f"St{i}", [128, D], F32) for i in range(2)]
        Sb = [sbt(f"Sb{i}", [128, D], BF16) for i in range(2)]
        sm = sbt("sm", [128, 4, 160], F32)
        pbuf = sbt("pbuf", [128, 256], F32)
        pb16 = sbt("pb16", [128, 256], BF16)
        YW = sbt("YW", [128, 8 * TT + 16 * 128], BF16)
        yTall = YW[:, 0:8 * TT].rearrange("p (k t) -> p k t", t=TT)
        wT = YW[:, 8 * TT:8 * TT + 2048].rearrange("p (j t) -> p j t", t=128)
        dgx = YW[:].rearrange("p (k t) -> p k t", t=128)
        pTall = sbt("pTall", [128, 2, TT], BF16)

        PB = pst("PB", [128, 6 * 512], F32)
        PTb = [pst(f"PT{i}", [128, 8, 128], BF16) for i in range(2)]
        bank = lambda i: PB[:, i * 512:(i + 1) * 512]
        bb = [Buf(f"bank{i}") for i in range(6)]
        ptb = [Buf(f"pt{i}") for i in range(2)]

        x_tokD = [x_tok, pTall[:].rearrange("p a t -> p (a t)")[:, 0:D]]
        B_tokD = [B_tok, B_tok2]
        wTD = [wT, sg[:].rearrange("p a t -> p (a t)").bitcast(BF16).rearrange("p (j t) -> p j t", t=128)]
        def C(name):
            o, w = CL.off[name]
            return cst[:, o:o + w]
        b_cst, b_msk, b_flg, b_idb, b_idf, b_abc = (Buf(n) for n in ("cst", "msk", "flg", "idb", "idf", "abc"))
        hb = [Buf(f"h{c}") for c in range(4)]
        xsb = [Buf(f"xsbf{c}") for c in range(4)]
        nsb = [Buf(f"nst{c}") for c in range(4)]
        xnTb = [Buf(f"xnT{c}") for c in range(4)]
        sgb = [Buf("sg0"), Buf("sg1")]
        ringb = [Buf(f"ring{i}") for i in range(RS)]
        wdb = [[Buf(f"wd{q}_{p}") for p in range(3)] for q in range(2)]
        smb = [Buf(f"sm{c}") for c in range(4)]
        b_ahist, b_xhist, b_alast, b_xlast = Buf("ahist"), Buf("xhist"), Buf("alast"), Buf("xlast")
        b_wT, b_cbm, b_xtok, b_Btok, b_xw, b_ybuf, b_xd, b_CTm = (
            Buf(n) for n in ("wT", "cbm", "xtok", "Btok", "xw", "ybuf", "xd", "CTm"))
        wTb = [b_wT]
        b_xtok1, b_Btok1, b_wT1 = Buf("xtok1"), Buf("Btok1"), Buf("wT1")
        xwb = [b_xw, Buf("xw1")]
        yTb = [Buf(f"yT{c}") for c in range(4)]
        pTb = [Buf(f"pT{c}") for c in range(4)]
        Stb = [Buf("St0"), Buf("St1")]
        Sbb = [Buf("Sb0"), Buf("Sb1")]
        b_pbuf, b_pb16, b_stg = Buf("pbuf"), Buf("pb16"), Buf("stg")

        class Region:
            def __init__(self):
                self.bufs = []

            def phase(self, names):
                fence = []
                for b in self.bufs:
                    if b.last_w is not None:
                        fence.append(b.last_w)
                    fence.extend(b.readers)
                self.bufs = [Buf(n, fence) for n in names]
                return self.bufs
        RH, RM, RA, RR, RYW, RYX, RB = (Region() for _ in range(7))
        xsT_p = yx[:].bitcast(BF16).rearrange("p (k t) -> p k t", t=TT)
        dg = Rt[:].rearrange("p j t -> p (j t)").bitcast(BF16)[:, 0:31 * 128].rearrange("p (k t) -> p k t", t=128)

        SS, RSTD, TMP1 = 0, 1, 2
        DT0, DTA0, ACOL, AEND, NACOL, EAC, WEND, EEND0, EEND1, TMP16 = 16, 32, 48, 64, 80, 96, 112, 128, 144, 0

        wait_all = []

        S.dma("sp", lambda e: e.dma_start(out=cst[:], in_=consts_d), writes=[b_cst])
        S.dma("sp", lambda e: e.dma_start(out=msk[:], in_=masks_d), writes=[b_msk])
        S.dma("sp", lambda e: e.dma_start(out=flg[:], in_=flags), writes=[b_flg])

        for t_, b_ in ((ident_b, b_idb), (ident_f, b_idf)):
            S.op("pool", lambda e, t_=t_: e.memset(t_[:], 0.0), writes=[b_])
            S.op("pool", lambda e, t_=t_: e.affine_select(out=t_[:], in_=t_[:], pattern=[[-1, 128]], compare_op=ALU.not_equal,
                                                          fill=1.0, base=0, channel_multiplier=1),
                 reads=[b_], writes=[b_])
        S.op("act", lambda e: e.activation(out=abc[:], in_=C("alog"), func=AF.Exp), reads=[b_cst], writes=[b_abc])
        S.op("dve", lambda e: e.tensor_scalar(out=abc[:], in0=abc[:], scalar1=-1.0, scalar2=None, op0=ALU.mult),
             reads=[b_abc], writes=[b_abc])
        U_p, E_p, U_s, EB_s, E_s = msk[:, 0, :], msk[:, 1, :], msk[:, 2, :], msk[:, 3, :], [msk[:, 4, :], msk[:, 5, :]]

        ring_pos = [0]

        def ring_load(w_ap, ncols):
            i = ring_pos[0] % RS
            ring_pos[0] += 1
            S.dma("pool", lambda e: e.dma_start(out=ring[i][:, :, 0:ncols],
                                                in_=w_ap.rearrange("(kc p) n -> p kc n", p=128)),
                  writes=[ringb[i]])
            return ring[i], ringb[i]

        def wd_load(q, w_ap, nk, kofs=0):
            for p0 in range(0, nk, 8):
                pn = min(8, nk - p0)
                S.dma("pool", lambda e, p0=p0, pn=pn: e.dma_start(
                    out=WDq[q][:, kofs + p0:kofs + p0 + pn, :],
                    in_=w_ap[p0 * 128:(p0 + pn) * 128, :].rearrange("(kc p) n -> p kc n", p=128)),
                    writes=[wdb[q][(kofs + p0) // 8]])

        def rstd_stage(srcs, cs, width=D):
            for (ap, bufs), c in zip(srcs, cs):
                S.op("dve", lambda e, c=c: e.memset(nst[:, c, SS:SS + 1], 0.0), writes=[nsb[c]])
            for (ap, bufs), c in zip(srcs, cs):
                S.op("act", lambda e, c=c, ap=ap: e.activation(out=xsbf[:, c, 0:width], in_=ap, func=AF.Square,
                                                               accum_out=nst[:, c, SS:SS + 1]),
                     reads=list(bufs), writes=[nsb[c], xsb[c]])
            for (ap, bufs), c in zip(srcs, cs):
                S.op("act", lambda e, c=c: e.activation(out=nst[:, c, TMP1:TMP1 + 1], in_=nst[:, c, SS:SS + 1], func=AF.Sqrt,
                                                        bias=C("eps"), scale=1.0 / width),
                     reads=[nsb[c], b_cst], writes=[nsb[c]])
            for (ap, bufs), c in zip(srcs, cs):
                S.op("dve", lambda e, c=c: e.reciprocal(out=nst[:, c, RSTD:RSTD + 1], in_=nst[:, c, TMP1:TMP1 + 1]),
                     reads=[nsb[c]], writes=[nsb[c]])

        def rstd_of(src_ap, src_bufs, c, width=D):
            rstd_stage([(src_ap, src_bufs)], [c], width)

        xnTB = Rt[:].rearrange("p j t -> p (j t)").bitcast(BF16).rearrange("p (k t) -> p k t", t=TT)
        xnBb = [None] * 4

        def norms(nch, gname, sel="A", part=0):
            cs = list(range(nch))
            if part in (0, 1):
                rstd_stage([(h[:, c, :], [hb[c]]) for c in cs], cs)
                for c in cs:
                    if part == 1:
                        S.op("dve", lambda e, c=c: e.tensor_scalar(out=xsbf[:, c, :], in0=h[:, c, :], scalar1=nst[:, c, RSTD:RSTD + 1],
                                                                   scalar2=None, op0=ALU.mult),
                             reads=[hb[c], nsb[c]], writes=[xsb[c]])
                    else:
                        S.op("act", lambda e, c=c: e.activation(out=xsbf[:, c, :], in_=h[:, c, :], func=AF.Identity,
                                                                scale=nst[:, c, RSTD:RSTD + 1]),
                             reads=[hb[c], nsb[c]], writes=[xsb[c]])
            if part in (0, 2):
                if sel == "B":
                    xnBb[:] = RR.phase([f"xnB{c}" for c in range(4)])
                XT, XTb = (xnT, xnTb) if sel == "A" else (xnTB, xnBb)
                for c0 in range(0, nch, 2):
                    for c in cs[c0:c0 + 2]:
                        transpose_only(xsbf[:, c, :], xsb[c], c % 2)
                    for c in cs[c0:c0 + 2]:
                        scale_evac(XT[:, :, c * 128:(c + 1) * 128], XTb[c], gname, c % 2)

        def transpose_only(src, srcb, pti):
            def tr(e):
                for k in range(8):
                    i = e.transpose(out=PTb[pti][:, k, :], in_=src[:, k * 128:(k + 1) * 128], identity=ident_b[:])
                return i
            S.op("pe", tr, reads=[srcb, b_idb], writes=[ptb[pti]])

        def scale_evac(dst, dstb, gname, pti):
            g = C(gname).unsqueeze(2).broadcast_to([128, 8, 128])
            S.op("dve", lambda e: e.tensor_tensor(out=dst, in0=PTb[pti][:], in1=g, op=ALU.mult),
                 reads=[ptb[pti], b_cst], writes=[dstb])

        def transpose_scale(src, srcb, dst, dstb, gname, pti):
            def tr(e):
                for k in range(8):
                    i = e.transpose(out=PTb[pti][:, k, :], in_=src[:, k * 128:(k + 1) * 128], identity=ident_b[:])
                return i
            S.op("pe", tr, reads=[srcb, b_idb], writes=[ptb[pti]])
            g = C(gname).unsqueeze(2).broadcast_to([128, 8, 128])
            S.op("dve", lambda e: e.tensor_tensor(out=dst, in0=PTb[pti][:], in1=g, op=ALU.mult),
                 reads=[ptb[pti], b_cst], writes=[dstb])

        def ffn(nch, wg, wu, wd, gname, do_norm=True, hooks=None, sel="A"):
            n = nch * 128
            hid = regH[:].rearrange("p (f t) -> p f t", t=TT)
            hidb = RH.phase([f"hid{f}" for f in range(NFC)])
            if do_norm:
                norms(nch, gname, sel)
            XT, XTb = (xnT, xnTb) if sel == "A" else (xnTB, xnBb)
            xr = [XTb[c] for c in range(nch)]
            for u in range(NFC // 2):
                if hooks and u in hooks:
                    hooks[u]()
                tg, tgb = ring_load(wg[:, u * 256:(u + 1) * 256], 256)
                tu, tub = ring_load(wu[:, u * 256:(u + 1) * 256], 256)
                for fi in range(2):
                    f = 2 * u + fi
                    pg, pu = bank(f % 2), bank(2 + f % 2)

                    def mmg(e, t=tg, fi=fi, p=pg):
                        for k in range(8):
                            i = e.matmul(p[:, 0:n], lhsT=t[:, k, fi * 128:(fi + 1) * 128], rhs=XT[:, k, 0:n],
                                         start=(k == 0), stop=(k == 7))
                        return i
                    S.op("pe", mmg, reads=[tgb] + xr, writes=[bb[f % 2]])
                    S.op("pe", lambda e, t=tu, fi=fi, p=pu, mmg=mmg: mmg(e, t, fi, p), reads=[tub] + xr, writes=[bb[2 + f % 2]])
                    S.op("act", lambda e, f=f, p=pg: e.activation(out=sg[:, f % 2, 0:n], in_=p[:, 0:n], func=AF.Silu),
                         reads=[bb[f % 2]], writes=[sgb[f % 2]])
                    S.op("dve", lambda e, f=f, p=pu: e.tensor_tensor(out=hid[:, f, 0:n], in0=sg[:, f % 2, 0:n],
                                                                     in1=p[:, 0:n], op=ALU.mult),
                         reads=[sgb[f % 2], bb[2 + f % 2]], writes=[hidb[f]])
            for qd in range(4):
                wd_load(qd % 2, wd[:, qd * 256:(qd + 1) * 256], NFC)
                for c in range(nch):
                    po, pob = bank(4 + c % 2), bb[4 + c % 2]

                    def mmd(e, c=c, qd=qd, po=po):
                        for f in range(NFC):
                            i = e.matmul(po[:, 0:256], lhsT=hid[:, f, c * 128:(c + 1) * 128], rhs=WDq[qd % 2][:, f, :],
                                         start=(f == 0), stop=(f == NFC - 1))
                        return i
                    S.op("pe", mmd, reads=hidb + wdb[qd % 2], writes=[pob])
                    hs = h[:, c, qd * 256:(qd + 1) * 256]
                    S.op("dve", lambda e, po=po, hs=hs: e.scalar_tensor_tensor(out=hs, in0=po[:, 0:256], scalar=0.5,
                                                                                in1=hs, op0=ALU.mult, op1=ALU.add),
                         reads=[pob, hb[c]], writes=[hb[c]])

        def mixer(nch, segs, mode, fcol, first, nseq, smp, hook_after_norm=None, hook_after_ssd1=None, need_last=True):
            n = nch * 128
            w_in = W["w_in"]
            norms(nch, "g_mix")
            if hook_after_norm is not None:
                hook_after_norm()
            xr = [xnTb[c] for c in range(nch)]
            cT = regH[:, 0:8 * TT].rearrange("p (f t) -> p f t", t=TT)
            xbcT = regH[:, 8 * TT:20 * TT].rearrange("p (f t) -> p f t", t=TT)
            do_conf = mode in ("full", "prefix_last")
            full = mode == "full"
            if full:
                rh = RH.phase([f"cT{i}" for i in range(8)] + [f"xbcT{i}" for i in range(12)])
                cTb, xbcTb = rh[0:8], rh[8:20]
                XB = lambda xc: xbcT[:, xc, :]
            else:
                xbcTb = RYX.phase([f"xsTp{i}" for i in range(8)]) + RB.phase([f"bcTp{i}" for i in range(4)])
                XB = lambda xc: (xsT_p[:, xc, :] if xc < 8 else bcT_p[:, xc - 8, :])
            AW = 30 + TT if not smp else 2 * (30 + 64)
            a_pad = regA[:, 0:8 * AW].rearrange("p (c t) -> p c t", t=AW)
            XW = 3 + TT if not smp else 2 * (3 + 64)
            xbcpre = regM[:].bitcast(BF16)[:, 0:12 * XW].rearrange("p (c t) -> p c t", t=XW)

            def P0(si):
                return si * (30 + 64) if smp else 0

            def X0(si):
                return si * (3 + 64) if smp else 0

            if do_conf:
                apb = RA.phase([f"apad{i}" for i in range(8)])
                convy = regM[:, 0:8 * TT].rearrange("p (c t) -> p c t", t=TT)
                ysq = regM[:, 8 * TT:10 * TT].rearrange("p (a t) -> p a t", t=TT)
                lnm = regM[:, 10 * TT:11 * TT]
                lnr = regM[:, 11 * TT:12 * TT]
                rm = RM.phase([f"convy{i}" for i in range(8)] + ["ysq0", "ysq1", "lnm", "lnr"])
                convyb, b_ysq, b_lnm, b_lnr = rm[0:8], rm[8:10], rm[10], rm[11]
                if not smp:
                    if first or mode == "prefix_last":
                        S.op("dve", lambda e: e.memset(a_pad[:, :, 0:30], 0.0), writes=apb)
                    else:
                        S.op("act", lambda e: e.activation(out=a_pad[:, :, 0:30], in_=a_hist[:], func=AF.Copy),
                             reads=[b_ahist], writes=apb)
                else:
                    for si in range(2):
                        load_hist_T(sconv[si], 30, 8, a_pad, P0(si), apb)
                for u in range(4):
                    tv, tvb = ring_load(w_in[:, u * 256:(u + 1) * 256], 256)
                    tg, tgb = ring_load(w_in[:, 1024 + u * 256:1024 + (u + 1) * 256], 256)
                    for fi in range(2):
                        ch = 2 * u + fi
                        pv, pg = bank(ch % 2), bank(2 + ch % 2)

                        def mm(e, t=tv, fi=fi, p=pv):
                            for k in range(8):
                                i = e.matmul(p[:, 0:n], lhsT=t[:, k, fi * 128:(fi + 1) * 128], rhs=xnT[:, k, 0:n],
                                             start=(k == 0), stop=(k == 7))
                            return i
                        S.op("pe", mm, reads=[tvb] + xr, writes=[bb[ch % 2]])
                        S.op("pe", lambda e, t=tg, fi=fi, p=pg, mm=mm: mm(e, t, fi, p), reads=[tgb] + xr, writes=[bb[2 + ch % 2]])
                        S.op("act", lambda e, ch=ch, p=pg: e.activation(out=sg[:, ch % 2, 0:n], in_=p[:, 0:n],
                                                                        func=AF.Sigmoid),
                             reads=[bb[2 + ch % 2]], writes=[sgb[ch % 2]])
                        for si, (lo, sn) in enumerate(segs):
                            S.op("dve", lambda e, ch=ch, p=pv, lo=lo, sn=sn, si=si: e.tensor_tensor(
                                out=a_pad[:, ch, P0(si) + 30:P0(si) + 30 + sn], in0=p[:, lo:lo + sn],
                                in1=sg[:, ch % 2, lo:lo + sn], op=ALU.mult),
                                reads=[bb[ch % 2], sgb[ch % 2]], writes=[apb[ch]])
                            if need_last:
                                S.op("dve", lambda e, ch=ch, p=pv, lo=lo, sn=sn, si=si: e.tensor_tensor(
                                    out=a_last[:, si, ch, :], in0=p[:, lo + sn - 32:lo + sn],
                                    in1=sg[:, ch % 2, lo + sn - 32:lo + sn], op=ALU.mult),
                                    reads=[bb[ch % 2], sgb[ch % 2]], writes=[b_alast])
                if not smp:
                    if fcol is not None:
                        S.op("act", lambda e: e.activation(out=a_hist[:], in_=a_pad[:, :, TT:TT + 30], func=AF.Identity,
                                                           scale=fcol),
                             reads=apb + [b_flg], writes=[b_ahist])
                    else:
                        S.op("act", lambda e: e.activation(out=a_hist[:], in_=a_pad[:, :, TT:TT + 30], func=AF.Copy),
                             reads=apb, writes=[b_ahist])
            if full:
                cw = C("cw").rearrange("p (c k) -> p c k", k=31)
                stats_pending = []
                b_dgA, b_dgB = RR.phase(["dgA", "dgB"])
                for ch in range(8):
                    S.op("dve", lambda e, ch=ch: e.tensor_tensor(
                        out=dg[:, 0:16, :], in0=ident_b[:].unsqueeze(1).broadcast_to([128, 16, 128]),
                        in1=cw[:, ch, 0:16].unsqueeze(2).broadcast_to([128, 16, 128]), op=ALU.mult),
                        reads=[b_idb, b_cst], writes=[b_dgA])
                    S.op("dve", lambda e, ch=ch: e.tensor_tensor(
                        out=dg[:, 16:31, :], in0=ident_b[:].unsqueeze(1).broadcast_to([128, 15, 128]),
                        in1=cw[:, ch, 16:31].unsqueeze(2).broadcast_to([128, 15, 128]), op=ALU.mult),
                        reads=[b_idb, b_cst], writes=[b_dgB])
                    pc, pcb = bank(4 + ch % 2), bb[4 + ch % 2]

                    def cv(e, ch=ch, pc=pc, k0=0, k1=16):
                        for si, (lo, sn) in enumerate(segs):
                            for k in range(k0, k1):
                                i = e.matmul(pc[:, lo:lo + sn], lhsT=dg[:, k, :],
                                             rhs=a_pad[:, ch, P0(si) + k:P0(si) + k + sn],
                                             start=(k == 0), stop=(k == 30))
                        return i
                    if len(segs) == 1:
                        S.op("pe", cv, reads=[b_dgA, apb[ch]], writes=[pcb])
                        S.op("pe", lambda e, ch=ch, pc=pc, cv=cv: cv(e, ch, pc, 16, 31), reads=[b_dgB, apb[ch]], writes=[pcb])
                    else:
                        S.op("pe", lambda e, ch=ch, pc=pc, cv=cv: cv(e, ch, pc, 0, 31), reads=[b_dgA, b_dgB, apb[ch]], writes=[pcb])
                    while len(stats_pending) > 0 and stats_pending[0] < ch:
                        stats(stats_pending.pop(0))
                    cbo = CL.off["cb"][0]
                    S.op("act", lambda e, ch=ch, pc=pc: e.activation(out=convy[:, ch, 0:n], in_=pc[:, 0:n], func=AF.Identity,
                                                                     bias=cst[:, cbo + ch:cbo + ch + 1]),
                         reads=[pcb, b_cst], writes=[convyb[ch]])
                    S.op("act", lambda e, ch=ch: e.activation(out=ysq[:, ch % 2, 0:n], in_=convy[:, ch, 0:n], func=AF.Square),
                         reads=[convyb[ch]], writes=[b_ysq[ch % 2]])
                    def stats(ch):
                        S.op("pe", lambda e, ch=ch: e.matmul(bank(0)[:, 0:n], lhsT=E_p, rhs=convy[:, ch, 0:n],
                                                             start=(ch == 0), stop=(ch == 7)),
                             reads=[b_msk, convyb[ch]], writes=[bb[0]])
                        S.op("pe", lambda e, ch=ch: e.matmul(bank(1)[:, 0:n], lhsT=E_p, rhs=ysq[:, ch % 2, 0:n],
                                                             start=(ch == 0), stop=(ch == 7)),
                             reads=[b_msk, b_ysq[ch % 2]], writes=[bb[1]])
                    stats_pending.append(ch)
                    if not STATS_DELAY:
                        stats(stats_pending.pop(0))
                while stats_pending:
                    stats(stats_pending.pop(0))
                S.op("dve", lambda e: e.tensor_scalar(out=lnm[:, 0:n], in0=bank(0)[:, 0:n], scalar1=1.0 / 1024, scalar2=None,
                                                      op0=ALU.mult), reads=[bb[0]], writes=[b_lnm])
                S.op("dve", lambda e: e.tensor_tensor(out=lnr[:, 0:n], in0=lnm[:, 0:n], in1=lnm[:, 0:n], op=ALU.mult),
                     reads=[b_lnm], writes=[b_lnr])
                S.op("dve", lambda e: e.scalar_tensor_tensor(out=lnr[:, 0:n], in0=bank(1)[:, 0:n], scalar=1.0 / 1024,
                                                             in1=lnr[:, 0:n], op0=ALU.mult, op1=ALU.subtract),
                     reads=[bb[1], b_lnr], writes=[b_lnr])
                S.op("act", lambda e: e.activation(out=lnr[:, 0:n], in_=lnr[:, 0:n], func=AF.Sqrt, bias=C("eps")),
                     reads=[b_lnr, b_cst], writes=[b_lnr])
                S.op("dve", lambda e: e.reciprocal(out=lnr[:, 0:n], in_=lnr[:, 0:n]), reads=[b_lnr], writes=[b_lnr])
                lgo, lbo = CL.off["lng"][0], CL.off["lnb"][0]
                for ch in range(8):
                    S.op("dve", lambda e, ch=ch: e.tensor_tensor(out=convy[:, ch, 0:n], in0=convy[:, ch, 0:n],
                                                                 in1=lnm[:, 0:n], op=ALU.subtract),
                         reads=[convyb[ch], b_lnm], writes=[convyb[ch]])
                    S.op("dve", lambda e, ch=ch: e.tensor_tensor(out=convy[:, ch, 0:n], in0=convy[:, ch, 0:n],
                                                                 in1=lnr[:, 0:n], op=ALU.mult),
                         reads=[convyb[ch], b_lnr], writes=[convyb[ch]])
                    S.op("act", lambda e, ch=ch: e.activation(out=cT[:, ch, 0:n], in_=convy[:, ch, 0:n], func=AF.Silu,
                                                              bias=cst[:, lbo + ch:lbo + ch + 1],
                                                              scale=cst[:, lgo + ch:lgo + ch + 1]),
                         reads=[convyb[ch], b_cst], writes=[cTb[ch]])

            rm2 = RM.phase([f"xpre{i}" for i in range(12)])
            xpb = rm2
            if not smp:
                if first:
                    S.op("dve", lambda e: e.memset(xbcpre[:, :, 0:3], 0.0), writes=xpb)
                else:
                    S.op("act", lambda e: e.activation(out=xbcpre[:, :, 0:3], in_=x_hist[:], func=AF.Copy),
                         reads=[b_xhist], writes=xpb)
            else:
                for si in range(2):
                    load_hist_T(sxbc[si], 3, 12, xbcpre, X0(si), xpb)
            swv = C("sw").rearrange("p (c k) -> p c k", k=4)
            sbo = CL.off["sb"][0]
            b_dgx, = RYW.phase(["dgx"])
            for xc in range(12):
                S.op("dve", lambda e, xc=xc: e.tensor_tensor(
                    out=dgx[:, 4 * xc:4 * xc + 4, :], in0=ident_b[:].unsqueeze(1).broadcast_to([128, 4, 128]),
                    in1=swv[:, xc, :].unsqueeze(2).broadcast_to([128, 4, 128]), op=ALU.mult),
                    reads=[b_idb, b_cst], writes=[b_dgx])

            def short_conv(xc):
                pc, pcb = bank(4 + xc % 2), bb[4 + xc % 2]

                def cvx(e, xc=xc, pc=pc):
                    for si, (lo, sn) in enumerate(segs):
                        for k in range(4):
                            i = e.matmul(pc[:, lo:lo + sn], lhsT=dgx[:, 4 * xc + k, :],
                                         rhs=xbcpre[:, xc, X0(si) + k:X0(si) + k + sn], start=(k == 0), stop=(k == 3))
                    return i
                S.op("pe", cvx, reads=[b_dgx, xpb[xc]], writes=[pcb])
                S.op("act", lambda e, xc=xc, pc=pc: e.activation(out=XB(xc)[:, 0:n], in_=pc[:, 0:n], func=AF.Silu,
                                                                 bias=cst[:, sbo + xc:sbo + xc + 1]),
                     reads=[pcb, b_cst], writes=[xbcTb[xc]])

            pend = []
            for u in range(6):
                if u == 1 and hook_after_ssd1 is not None:
                    hook_after_ssd1(1)
                if u == 4 and hook_after_ssd1 is not None:
                    hook_after_ssd1(2)
                tx, txb = ring_load(w_in[:, 3072 + u * 256:3072 + (u + 1) * 256], 256)
                for fi in range(2):
                    xc = 2 * u + fi
                    px, pxb = bank(xc % 4), bb[xc % 4]

                    def mm(e, t=tx, fi=fi, p=px):
                        for k in range(8):
                            i = e.matmul(p[:, 0:n], lhsT=t[:, k, fi * 128:(fi + 1) * 128], rhs=xnT[:, k, 0:n],
                                         start=(k == 0), stop=(k == 7))
                        return i
                    S.op("pe", mm, reads=[txb] + xr, writes=[pxb])
                    for si, (lo, sn) in enumerate(segs):
                        S.op("act", lambda e, xc=xc, p=px, lo=lo, sn=sn, si=si: e.activation(
                            out=xbcpre[:, xc, X0(si) + 3:X0(si) + 3 + sn], in_=p[:, lo:lo + sn], func=AF.Copy),
                            reads=[pxb], writes=[xpb[xc]])
                        if need_last:
                            S.op("act", lambda e, xc=xc, p=px, lo=lo, sn=sn, si=si: e.activation(
                                out=x_last[:, si, xc, :], in_=p[:, lo + sn - 4:lo + sn], func=AF.Copy),
                                reads=[pxb], writes=[b_xlast])
                    pend.append(xc)
                    if len(pend) > 2:
                        short_conv(pend.pop(0))
            while pend:
                short_conv(pend.pop(0))
            if not smp:
                if fcol is not None:
                    S.op("act", lambda e: e.activation(out=x_hist[:], in_=xbcpre[:, :, TT:TT + 3], func=AF.Identity, scale=fcol),
                         reads=xpb + [b_flg], writes=[b_xhist])
                else:
                    S.op("act", lambda e: e.activation(out=x_hist[:], in_=xbcpre[:, :, TT:TT + 3], func=AF.Copy),
                         reads=xpb, writes=[b_xhist])

            def do_dt():
                tdt, tdtb = ring_load(w_in[:, 4608:4624], 16)
                for c in range(nch):
                    pdt = bank(4)[:, c * 16:(c + 1) * 16]

                    def mm(e, c=c, pdt=pdt):
                        for k in range(8):
                            i = e.matmul(pdt, lhsT=xnT[:, k, c * 128:(c + 1) * 128], rhs=tdt[:, k, 0:16],
                                         start=(k == 0), stop=(k == 7))
                        return i
                    S.op("pe", mm, reads=[tdtb, xnTb[c]], writes=[bb[4]])
                for c in range(nch):
                    S.op("dve", lambda e, c=c: e.tensor_tensor(out=sm[:, c, DT0:DT0 + 16], in0=bank(4)[:, c * 16:(c + 1) * 16],
                                                               in1=C("dtb"), op=ALU.add),
                         reads=[bb[4], b_cst], writes=[smb[c]])
                for c in range(nch):
                    S.op("act", lambda e, c=c: e.activation(out=sm[:, c, DT0:DT0 + 16], in_=sm[:, c, DT0:DT0 + 16], func=AF.Exp),
                         reads=[smb[c]], writes=[smb[c]])
                for c in range(nch):
                    S.op("act", lambda e, c=c: e.activation(out=sm[:, c, DT0:DT0 + 16], in_=sm[:, c, DT0:DT0 + 16], func=AF.Ln,
                                                            bias=C("one")),
                         reads=[smb[c], b_cst], writes=[smb[c]])
                for c in range(nch):
                    S.op("dve", lambda e, c=c: e.tensor_tensor(out=sm[:, c, DTA0:DTA0 + 16], in0=sm[:, c, DT0:DT0 + 16],
                                                               in1=abc[:], op=ALU.mult),
                         reads=[smb[c], b_abc], writes=[smb[c]])

            if not full:
                holder = [None]

                do_dt()
                PREFIX_DEFER = ssd_prefix(nch, XB, xbcTb, fcol, None)
            else:
                do_dt()

            if full:
                sz = regM[:, 0:4 * D].rearrange("p (c d) -> p c d", d=D)
                szb = RM.phase([f"sz{c}" for c in range(4)])
                zu = [ring_load(w_in[:, 2048 + u * 256:2048 + (u + 1) * 256], 256) for u in range(4)]
                for c in range(nch):
                    for hh in range(2):
                        pz, pzb = bank(hh), bb[hh]

                        def mm(e, c=c, hh=hh, pz=pz):
                            for uu in range(2):
                                t = zu[2 * hh + uu][0]
                                for k in range(8):
                                    i = e.matmul(pz[:, uu * 256:(uu + 1) * 256], lhsT=xnT[:, k, c * 128:(c + 1) * 128],
                                                 rhs=t[:, k, :], start=(k == 0), stop=(k == 7))
                            return i
                        S.op("pe", mm, reads=[zu[2 * hh][1], zu[2 * hh + 1][1], xnTb[c]], writes=[pzb])
                        S.op("act", lambda e, c=c, hh=hh, pz=pz: e.activation(out=sz[:, c, hh * 512:(hh + 1) * 512], in_=pz,
                                                                              func=AF.Silu),
                             reads=[pzb], writes=[szb[c]])

            if not full:
                return PREFIX_DEFER

            Um, EBm = (U_s, EB_s) if smp else (U_p, E_p)
            b_R, = RR.phase(["R"])
            _b = RYW.phase([f"yT{c}" for c in range(4)] + ["wT"])
            yTb[:] = _b[0:4]
            wTb[0] = _b[4]
            def chunk_body(c):
                smc = sm[:, c, :]
                cr = slice(c * 128, (c + 1) * 128)
                pi = c % 2
                xtk, xtdeps = x_tokD[pi], ([b_xtok] if pi == 0 else [b_xtok1] + pTb)
                Btk, btdep = B_tokD[pi], (b_Btok if pi == 0 else b_Btok1)
                wTc, wTdeps = wTD[pi], ([wTb[0]] if pi == 0 else [b_wT1] + sgb)
                def trx(e, cr=cr):
                    for j in range(8):
                        i = e.transpose(out=PTb[0][:, j, :], in_=XB(j)[:, cr], identity=ident_b[:])
                    return i
                S.op("pe", trx, reads=xbcTb[0:8] + [b_idb], writes=[ptb[0]])
                S.op("act", lambda e: e.activation(out=xtk[:], in_=PTb[0][:].rearrange("p a b -> p (a b)"), func=AF.Copy),
                     reads=[ptb[0]], writes=xtdeps)

                def trb(e, cr=cr):
                    for g in range(2):
                        i = e.transpose(out=PTb[1][:, g, :], in_=XB(8 + g)[:, cr], identity=ident_b[:])
                    return i
                S.op("pe", trb, reads=xbcTb[8:10] + [b_idb], writes=[ptb[1]])
                S.op("act", lambda e: e.activation(out=Btk[:], in_=PTb[1][:, 0:2, :].rearrange("p a b -> p (a b)"),
                                                   func=AF.Copy),
                     reads=[ptb[1]], writes=[btdep])
                pac = bank(5)

                def mac(e, smc=smc):
                    e.matmul(pac[:, 0:16], lhsT=Um, rhs=smc[:, DTA0:DTA0 + 16], start=True, stop=True)
                    i = e.matmul(pac[:, 16:32], lhsT=EBm, rhs=smc[:, DTA0:DTA0 + 16], start=True, stop=True)
                    if smp:
                        for q in range(2):
                            i = e.matmul(pac[:, 32 + 16 * q:48 + 16 * q], lhsT=E_s[q], rhs=smc[:, DTA0:DTA0 + 16],
                                         start=True, stop=True)
                    return i
                S.op("pe", mac, reads=[b_msk, smb[c]], writes=[bb[5]])
                S.op("dve", lambda e, smc=smc: e.tensor_copy(out=smc[:, ACOL:ACOL + 32], in_=pac[:, 0:32]),
                     reads=[bb[5]], writes=[smb[c]])
                if smp:
                    S.op("dve", lambda e, smc=smc: e.tensor_copy(out=smc[:, EEND0:EEND0 + 32], in_=pac[:, 32:64]),
                         reads=[bb[5]], writes=[smb[c]])
                else:
                    S.op("dve", lambda e, smc=smc: e.tensor_copy(out=smc[:, EEND0:EEND0 + 16], in_=pac[:, 16:32]),
                         reads=[bb[5]], writes=[smb[c]])
                S.op("dve", lambda e, smc=smc: e.tensor_tensor(out=smc[:, WEND:WEND + 16], in0=smc[:, AEND:AEND + 16],
                                                               in1=smc[:, ACOL:ACOL + 16], op=ALU.subtract),
                     reads=[smb[c]], writes=[smb[c]])
                S.op("act", lambda e, smc=smc: e.activation(out=smc[:, WEND:WEND + 16], in_=smc[:, WEND:WEND + 16], func=AF.Exp),
                     reads=[smb[c]], writes=[smb[c]])
                S.op("dve", lambda e, smc=smc: e.tensor_tensor(out=smc[:, WEND:WEND + 16], in0=smc[:, WEND:WEND + 16],
                                                               in1=smc[:, DT0:DT0 + 16], op=ALU.mult),
                     reads=[smb[c]], writes=[smb[c]])
                S.op("act", lambda e, smc=smc: e.activation(out=smc[:, EEND0:EEND0 + 16 * nseq], in_=smc[:, EEND0:EEND0 + 16 * nseq],
                                                            func=AF.Exp),
                     reads=[smb[c]], writes=[smb[c]])
                xt3 = xtk[:].rearrange("p (j q) -> p j q", q=64)
                if full:
                    def mcb(e, cr=cr):
                        for g in range(2):
                            i = e.matmul(bank(4)[:, 64 + g * 128:64 + (g + 1) * 128], lhsT=XB(8 + g)[:, cr],
                                         rhs=XB(10 + g)[:, cr], start=True, stop=True)
                        return i
                    S.op("pe", mcb, reads=xbcTb[8:12], writes=[bb[4]])
                    S.op("dve", lambda e: e.tensor_tensor(
                        out=cbm[:], in0=bank(4)[:, 64:320].rearrange("p (g t) -> p g t", t=128),
                        in1=Um.unsqueeze(1).broadcast_to([128, 2, 128]), op=ALU.mult),
                        reads=[bb[4], b_msk], writes=[b_cbm])
                    S.op("dve", lambda e, smc=smc: e.tensor_tensor(
                        out=Rt[:], in0=smc[:, DTA0:DTA0 + 16].unsqueeze(2).broadcast_to([128, 16, 128]),
                        in1=Um.unsqueeze(1).broadcast_to([128, 16, 128]), op=ALU.mult),
                        reads=[smb[c], b_msk], writes=[b_R])
                    Rf = Rt[:].rearrange("p j t -> p (j t)")
                    for qb in range(4):
                        S.op("pe", lambda e, qb=qb: e.matmul(bank(qb), lhsT=E_p, rhs=Rf[:, qb * 512:(qb + 1) * 512],
                                                             start=True, stop=True),
                             reads=[b_msk, b_R], writes=[bb[qb]])
                    for qb in range(4):
                        S.op("dve", lambda e, qb=qb, smc=smc: e.tensor_tensor(
                            out=Rt[:, 4 * qb:4 * qb + 4, :], in0=bank(qb).rearrange("p (j t) -> p j t", t=128),
                            in1=smc[:, ACOL + 4 * qb:ACOL + 4 * qb + 4].unsqueeze(2).broadcast_to([128, 4, 128]),
                            op=ALU.min), reads=[bb[qb], smb[c]], writes=[b_R])
                    S.op("dve", lambda e, smc=smc: e.tensor_tensor(
                        out=Rt[:], in0=Rt[:], in1=smc[:, ACOL:ACOL + 16].unsqueeze(2).broadcast_to([128, 16, 128]),
                        op=ALU.subtract), reads=[b_R, smb[c]], writes=[b_R])
                    S.op("act", lambda e: e.activation(out=Rf, in_=Rf, func=AF.Exp), reads=[b_R], writes=[b_R])
                    S.op("dve", lambda e, smc=smc: e.tensor_tensor(
                        out=Rt[:], in0=Rt[:], in1=smc[:, DT0:DT0 + 16].unsqueeze(2).broadcast_to([128, 16, 128]),
                        op=ALU.mult), reads=[b_R, smb[c]], writes=[b_R])
                    S.op("dve", lambda e: e.tensor_tensor(
                        out=wTc[:].rearrange("p (g j) t -> p g j t", g=2), in0=Rt[:].rearrange("p (g j) t -> p g j t", g=2),
                        in1=cbm[:].unsqueeze(2).broadcast_to([128, 2, 8, 128]), op=ALU.mult),
                        reads=[b_R, b_cbm], writes=wTdeps)
                    def mmy(e):
                        for j in range(16):
                            i = e.matmul(PB[:, j * 64:(j + 1) * 64], lhsT=wTc[:, j, :], rhs=xtk[:, j * 64:(j + 1) * 64],
                                         start=True, stop=True)
                        return i
                    S.mark()
                    S.op("pe", mmy, reads=wTdeps + xtdeps, writes=[bb[0], bb[1]])
                    if smp:
                        S.op("dve", lambda e: e.memset(CTm[:], 0.0), writes=[b_CTm])
                        for q in range(2):
                            for g in range(2):
                                S.op("dve", lambda e, q=q, g=g, c0=c * 128 + q * 64: e.tensor_copy(
                                    out=CTm[:, q, g, q * 64:(q + 1) * 64], in_=XB(10 + g)[:, c0:c0 + 64]),
                                    reads=[xbcTb[10 + g]], writes=[b_CTm])

                    def mmys(e, cr=cr):
                        for g in range(2):
                            for q in range(nseq):
                                lt = CTm[:, q, g, :] if smp else XB(10 + g)[:, cr]
                                i = e.matmul(PB[:, 1024 + g * 512:1024 + (g + 1) * 512], lhsT=lt,
                                             rhs=Sb[q][:, g * 512:(g + 1) * 512], start=(q == 0), stop=(q == nseq - 1))
                        return i
                    S.op("pe", mmys, reads=xbcTb[10:12] + [b_CTm] + Sbb[0:nseq], writes=[bb[2], bb[3]])
                    S.op("act", lambda e, smc=smc: e.activation(out=smc[:, EAC:EAC + 16], in_=smc[:, ACOL:ACOL + 16], func=AF.Exp),
                         reads=[smb[c]], writes=[smb[c]])
                    yb3 = ybuf[:].rearrange("p (j q) -> p j q", q=64)
                    for g in range(2):
                        S.op("dve", lambda e, g=g, smc=smc: e.tensor_tensor(
                            out=yb3[:, 8 * g:8 * g + 8, :], in0=bank(2 + g).rearrange("p (j q) -> p j q", q=64),
                            in1=smc[:, EAC + 8 * g:EAC + 8 * g + 8].unsqueeze(2).broadcast_to([128, 8, 64]), op=ALU.mult),
                            reads=[bb[2 + g], smb[c]], writes=[b_ybuf])
                    for g in range(2):
                        S.op("dve", lambda e, g=g: e.tensor_tensor(out=ybuf[:, g * 512:(g + 1) * 512],
                                                                   in0=ybuf[:, g * 512:(g + 1) * 512], in1=bank(g), op=ALU.add),
                             reads=[bb[g], b_ybuf], writes=[b_ybuf])
                    S.op("dve", lambda e: e.tensor_tensor(
                        out=xd[:].rearrange("p (j q) -> p j q", q=64), in0=xt3,
                        in1=C("dsk").unsqueeze(2).broadcast_to([128, 16, 64]), op=ALU.mult),
                        reads=xtdeps + [b_cst], writes=[b_xd])
                    S.op("dve", lambda e: e.tensor_tensor(out=ybuf[:], in0=ybuf[:], in1=xd[:], op=ALU.add),
                         reads=[b_ybuf, b_xd], writes=[b_ybuf])
                    S.op("dve", lambda e, c=c: e.tensor_tensor(out=ybuf[:], in0=ybuf[:], in1=sz[:, c, :], op=ALU.mult),
                         reads=[b_ybuf, szb[c]], writes=[b_ybuf])
                    rstd_of(ybuf[:], [b_ybuf], c)
                    S.op("act", lambda e, c=c: e.activation(out=xsbf[:, c, :], in_=ybuf[:], func=AF.Identity,
                                                            scale=nst[:, c, RSTD:RSTD + 1]),
                         reads=[b_ybuf, nsb[c]], writes=[xsb[c]])
                    transpose_scale(xsbf[:, c, :], xsb[c], yTall[:, :, cr], yTb[c], "g_ssd", 0)

                S.op("dve", lambda e, smc=smc: e.tensor_tensor(
                    out=xw[:].rearrange("p (j q) -> p j q", q=64), in0=xt3,
                    in1=smc[:, WEND:WEND + 16].unsqueeze(2).broadcast_to([128, 16, 64]), op=ALU.mult),
                    reads=xtdeps + [smb[c]], writes=[b_xw])
                for q in range(nseq):
                    rows = slice(q * 64, (q + 1) * 64) if smp else slice(0, 128)

                    def mms(e, rows=rows):
                        for g in range(2):
                            i = e.matmul(bank(4 + g), lhsT=Btk[rows, g * 128:(g + 1) * 128],
                                         rhs=xw[rows, g * 512:(g + 1) * 512], start=True, stop=True)
                        return i
                    S.op("pe", mms, reads=[btdep, b_xw], writes=[bb[4], bb[5]])
                    s3 = St[q][:].rearrange("p (j q) -> p j q", q=64)
                    S.op("dve", lambda e, s3=s3, smc=smc, q=q: e.tensor_tensor(
                        out=s3, in0=s3, in1=smc[:, EEND0 + 16 * q:EEND0 + 16 * q + 16].unsqueeze(2).broadcast_to([128, 16, 64]),
                        op=ALU.mult), reads=[Stb[q], smb[c]], writes=[Stb[q]])
                    for g in range(2):
                        S.op("dve", lambda e, q=q, g=g: e.tensor_tensor(out=St[q][:, g * 512:(g + 1) * 512],
                                                                        in0=St[q][:, g * 512:(g + 1) * 512], in1=bank(4 + g), op=ALU.add),
                             reads=[Stb[q], bb[4 + g]], writes=[Stb[q]])
                    if fcol is not None and c == nch - 1:
                        S.op("dve", lambda e, q=q: e.tensor_scalar(out=St[q][:], in0=St[q][:], scalar1=fcol, scalar2=None,
                                                                   op0=ALU.mult),
                             reads=[Stb[q], b_flg], writes=[Stb[q]])
                    S.op("act", lambda e, q=q: e.activation(out=Sb[q][:], in_=St[q][:], func=AF.Copy),
                         reads=[Stb[q]], writes=[Sbb[q]])


            parts = []
            for c in range(nch):
                S.capture_begin()
                chunk_body(c)
                ops = S.capture_end()
                k = ops.index(None)
                parts.append((ops[:k], ops[k + 1:]))
            for it in parts[0][0]:
                S.replay(it)
            for c in range(nch):
                Bq = list(parts[c][1])
                Aq = list(parts[c + 1][0]) if c + 1 < nch else []
                na, nb = len(Aq), len(Bq)
                ia = ib = 0
                while ia < na or ib < nb:
                    if ib < nb and (ia >= na or ib * na <= ia * nb):
                        S.replay(Bq[ib]); ib += 1
                    else:
                        S.replay(Aq[ia]); ia += 1

            if full:
                for qd in range(4):
                    wd_load(qd % 2, W["w_out"][:, qd * 256:(qd + 1) * 256], 16)
                    for c in range(nch):
                        cr = slice(c * 128, (c + 1) * 128)
                        po, pob = bank(c % 2), bb[c % 2]

                        def mmo(e, cr=cr, qd=qd, po=po):
                            for mc in range(8):
                                e.matmul(po[:, 0:256], lhsT=cT[:, mc, cr], rhs=WDq[qd % 2][:, mc, :], start=(mc == 0), stop=False)
                            for mc in range(8):
                                i = e.matmul(po[:, 0:256], lhsT=yTall[:, mc, cr], rhs=WDq[qd % 2][:, 8 + mc, :], start=False,
                                             stop=(mc == 7))
                            return i
                        S.op("pe", mmo, reads=cTb + [yTb[c]] + wdb[qd % 2], writes=[pob])
                        hs = h[:, c, qd * 256:(qd + 1) * 256]
                        S.op("dve", lambda e, po=po, hs=hs: e.tensor_tensor(out=hs, in0=hs, in1=po[:, 0:256], op=ALU.add),
                             reads=[pob, hb[c]], writes=[hb[c]])

        def ssd_prefix(nch, XB, xbcTb, fcol, hook_unused=None):
            xt4 = yTall[:].rearrange("p a t -> p (a t)").rearrange("p (c d) -> p c d", d=D)
            bt4 = wT[:].rearrange("p j t -> p (j t)")[:, 0:4 * 256].rearrange("p (c d) -> p c d", d=256)
            st = {}
            pac = bank(5)
            TAIL = NACOL

            def p_mac():
                _b = RYW.phase([f"xtk{c}" for c in range(4)] + [f"btk{c}" for c in range(4)])
                st["xtb"], st["btb"] = _b[0:4], _b[4:8]
                for c in range(nch):
                    smc = sm[:, c, :]
                    S.op("pe", lambda e, smc=smc, c=c: (e.matmul(pac[:, c * 64:c * 64 + 16], lhsT=U_p, rhs=smc[:, DTA0:DTA0 + 16],
                                                                 start=True, stop=True),
                                                        e.matmul(pac[:, c * 64 + 16:c * 64 + 32], lhsT=E_p, rhs=smc[:, DTA0:DTA0 + 16],
                                                                 start=True, stop=True))[1],
                         reads=[b_msk, smb[c]], writes=[bb[5]])
                for c in range(nch):
                    smc = sm[:, c, :]
                    S.op("dve", lambda e, smc=smc, c=c: e.tensor_copy(out=smc[:, ACOL:ACOL + 32], in_=pac[:, c * 64:c * 64 + 32]),
                         reads=[bb[5]], writes=[smb[c]])
                S.op("dve", lambda e: e.memset(sm[:, nch - 1, TAIL:TAIL + 16], 0.0), writes=[smb[nch - 1]])
                for c in range(nch - 2, -1, -1):
                    S.op("dve", lambda e, c=c: e.tensor_tensor(out=sm[:, c, TAIL:TAIL + 16], in0=sm[:, c + 1, TAIL:TAIL + 16],
                                                               in1=sm[:, c + 1, AEND:AEND + 16], op=ALU.add),
                         reads=[smb[c + 1]], writes=[smb[c]])
                S.op("dve", lambda e: e.tensor_tensor(out=sm[:, 0, EEND0:EEND0 + 16], in0=sm[:, 0, TAIL:TAIL + 16],
                                                      in1=sm[:, 0, AEND:AEND + 16], op=ALU.add),
                     reads=[smb[0]], writes=[smb[0]])
                for c in range(nch):
                    smc = sm[:, c, :]
                    S.op("dve", lambda e, smc=smc: e.tensor_tensor(out=smc[:, WEND:WEND + 16], in0=smc[:, AEND:AEND + 16],
                                                                   in1=smc[:, ACOL:ACOL + 16], op=ALU.subtract),
                         reads=[smb[c]], writes=[smb[c]])
                for c in range(nch):
                    smc = sm[:, c, :]
                    S.op("dve", lambda e, smc=smc: e.tensor_tensor(out=smc[:, WEND:WEND + 16], in0=smc[:, WEND:WEND + 16],
                                                                   in1=smc[:, TAIL:TAIL + 16], op=ALU.add),
                         reads=[smb[c]], writes=[smb[c]])
                for c in range(nch):
                    smc = sm[:, c, :]
                    S.op("act", lambda e, smc=smc: e.activation(out=smc[:, WEND:WEND + 16], in_=smc[:, WEND:WEND + 16], func=AF.Exp),
                         reads=[smb[c]], writes=[smb[c]])
                S.op("act", lambda e: e.activation(out=sm[:, 0, EEND0:EEND0 + 16], in_=sm[:, 0, EEND0:EEND0 + 16], func=AF.Exp),
                     reads=[smb[0]], writes=[smb[0]])
                for c in range(nch):
                    smc = sm[:, c, :]
                    S.op("dve", lambda e, smc=smc: e.tensor_tensor(out=smc[:, WEND:WEND + 16], in0=smc[:, WEND:WEND + 16],
                                                                   in1=smc[:, DT0:DT0 + 16], op=ALU.mult),
                         reads=[smb[c]], writes=[smb[c]])

            def p_tr(c):
                cr = slice(c * 128, (c + 1) * 128)

                def trx(e, cr=cr):
                    for j in range(8):
                        i = e.transpose(out=PTb[0][:, j, :], in_=XB(j)[:, cr], identity=ident_b[:])
                    return i
                S.op("pe", trx, reads=xbcTb[0:8] + [b_idb], writes=[ptb[0]])
                S.op("act", lambda e, c=c: e.activation(out=xt4[:, c, :], in_=PTb[0][:].rearrange("p a b -> p (a b)"), func=AF.Copy),
                     reads=[ptb[0]], writes=[st["xtb"][c]])

                def trb(e, cr=cr):
                    for g in range(2):
                        i = e.transpose(out=PTb[1][:, g, :], in_=XB(8 + g)[:, cr], identity=ident_b[:])
                    return i
                S.op("pe", trb, reads=xbcTb[8:10] + [b_idb], writes=[ptb[1]])
                S.op("act", lambda e, c=c: e.activation(out=bt4[:, c, :], in_=PTb[1][:, 0:2, :].rearrange("p a b -> p (a b)"),
                                                        func=AF.Copy),
                     reads=[ptb[1]], writes=[st["btb"][c]])

            def p_xw(c):
                xwc, xwcb = xw2[:, c % 2, :], xwb[c % 2]
                S.op("dve", lambda e, c=c, xwc=xwc: e.tensor_tensor(
                    out=xwc.rearrange("p (j q) -> p j q", q=64), in0=xt4[:, c, :].rearrange("p (j q) -> p j q", q=64),
                    in1=sm[:, c, WEND:WEND + 16].unsqueeze(2).broadcast_to([128, 16, 64]), op=ALU.mult),
                    reads=[st["xtb"][c], smb[c]], writes=[xwcb])

            def p_mm(c):
                xwc, xwcb = xw2[:, c % 2, :], xwb[c % 2]

                def mms(e, c=c, xwc=xwc):
                    for g in range(2):
                        i = e.matmul(bank(4 + g), lhsT=bt4[:, c, g * 128:(g + 1) * 128], rhs=xwc[:, g * 512:(g + 1) * 512],
                                     start=(c == 0), stop=(c == nch - 1))
                    return i
                S.op("pe", mms, reads=[st["btb"][c], xwcb], writes=[bb[4], bb[5]])

            def p_fin():
                s3 = St[0][:].rearrange("p (j q) -> p j q", q=64)
                S.op("dve", lambda e: e.tensor_tensor(out=s3, in0=s3,
                                                      in1=sm[:, 0, EEND0:EEND0 + 16].unsqueeze(2).broadcast_to([128, 16, 64]),
                                                      op=ALU.mult), reads=[Stb[0], smb[0]], writes=[Stb[0]])
                for g in range(2):
                    S.op("dve", lambda e, g=g: e.tensor_tensor(out=St[0][:, g * 512:(g + 1) * 512], in0=St[0][:, g * 512:(g + 1) * 512],
                                                               in1=bank(4 + g), op=ALU.add),
                         reads=[Stb[0], bb[4 + g]], writes=[Stb[0]])
                if fcol is not None:
                    S.op("dve", lambda e: e.tensor_scalar(out=St[0][:], in0=St[0][:], scalar1=fcol, scalar2=None, op0=ALU.mult),
                         reads=[Stb[0], b_flg], writes=[Stb[0]])
                S.op("act", lambda e: e.activation(out=Sb[0][:], in_=St[0][:], func=AF.Copy), reads=[Stb[0]], writes=[Sbb[0]])

            seq = lambda *fs: (lambda: [f() for f in fs])
            return {
                1: p_mac,
                2: lambda: p_tr(0), 3: lambda: p_tr(1), 4: lambda: p_tr(2), 5: lambda: p_tr(3),
                6: seq(lambda: p_xw(0), lambda: p_xw(1)),
                7: seq(lambda: p_mm(0), lambda: p_xw(2)),
                8: seq(lambda: p_mm(1), lambda: p_xw(3)),
                9: lambda: p_mm(2),
                10: seq(lambda: p_mm(3), p_fin),
            }

        def load_hist_T(src, nrow, nchunk, dst, col0, dstb):
            S.dma("sp", lambda e: e.dma_start(out=stg[0:nrow, 0:nchunk * 128], in_=src), writes=[b_stg, b_ybuf, b_xd])
            flat = stg
            for ch in range(nchunk):
                pt = bank(4 + ch % 2)
                S.op("pe", lambda e, ch=ch, pt=pt: e.transpose(out=pt[:, 0:nrow], in_=flat[0:nrow, ch * 128:(ch + 1) * 128],
                                                               identity=ident_f[0:nrow, 0:nrow]),
                     reads=[b_stg, b_ybuf, b_xd, b_idf], writes=[bb[4 + ch % 2]])
                S.op("act", lambda e, ch=ch, pt=pt: e.activation(out=dst[:, ch, col0:col0 + nrow], in_=pt[:, 0:nrow], func=AF.Copy),
                     reads=[bb[4 + ch % 2]], writes=[dstb[ch]])

        def ple_final(nch, p_src, y_dst):
            norms(nch, "g_ple")
            for c in range(nch):
                S.dma("sp", lambda e, c=c: e.dma_start(out=pbuf[:], in_=p_src[c * 128:(c + 1) * 128, :]), writes=[b_pbuf])
                S.op("dve", lambda e: e.tensor_copy(out=pb16[:], in_=pbuf[:]), reads=[b_pbuf], writes=[b_pb16])

                def trp(e):
                    for k in range(2):
                        i = e.transpose(out=PTb[1][:, k, :], in_=pb16[:, k * 128:(k + 1) * 128], identity=ident_b[:])
                    return i
                S.op("pe", trp, reads=[b_pb16, b_idb], writes=[ptb[1]])
                S.op("act", lambda e, c=c: e.activation(out=pTall[:, :, c * 128:(c + 1) * 128], in_=PTb[1][:, 0:2, :], func=AF.Copy),
                     reads=[ptb[1]], writes=[pTb[c]])
            for qd in range(4):
                wd_load(qd % 2, W["ple_w_gate"][:, qd * 256:(qd + 1) * 256], 8)
                wd_load(qd % 2, W["ple_w_proj"][:, qd * 256:(qd + 1) * 256], 2, kofs=8)
                for c in range(nch):
                    pg, pp = bank(c % 2), bank(2 + c % 2)

                    def mmg(e, c=c, qd=qd, pg=pg):
                        for k in range(8):
                            i = e.matmul(pg[:, 0:256], lhsT=xnT[:, k, c * 128:(c + 1) * 128], rhs=WDq[qd % 2][:, k, :],
                                         start=(k == 0), stop=(k == 7))
                        return i
                    S.op("pe", mmg, reads=[xnTb[c]] + wdb[qd % 2], writes=[bb[c % 2]])

                    def mmp(e, c=c, qd=qd, pp=pp):
                        for k in range(2):
                            i = e.matmul(pp[:, 0:256], lhsT=pTall[:, k, c * 128:(c + 1) * 128], rhs=WDq[qd % 2][:, 8 + k, :],
                                         start=(k == 0), stop=(k == 1))
                        return i
                    S.op("pe", mmp, reads=[pTb[c]] + wdb[qd % 2], writes=[bb[2 + c % 2]])
                    S.op("act", lambda e, c=c, pg=pg: e.activation(out=sg[:, c % 2, 0:256], in_=pg[:, 0:256], func=AF.Sigmoid),
                         reads=[bb[c % 2]], writes=[sgb[c % 2]])
                    S.op("dve", lambda e, c=c, pp=pp: e.tensor_tensor(out=sg[:, c % 2, 0:256], in0=sg[:, c % 2, 0:256],
                                                                      in1=pp[:, 0:256], op=ALU.mult),
                         reads=[sgb[c % 2], bb[2 + c % 2]], writes=[sgb[c % 2]])
                    hs = h[:, c, qd * 256:(qd + 1) * 256]
                    S.op("dve", lambda e, c=c, hs=hs: e.tensor_tensor(out=hs, in0=hs, in1=sg[:, c % 2, 0:256], op=ALU.add),
                         reads=[sgb[c % 2], hb[c]], writes=[hb[c]])
            cs = list(range(nch))
            rstd_stage([(h[:, c, :], [hb[c]]) for c in cs], cs)
            for c in cs:
                ob, obb = (xd, b_xd) if c % 2 == 0 else (ybuf, b_ybuf)
                S.op("dve", lambda e, c=c, ob=ob: e.scalar_tensor_tensor(out=ob[:], in0=h[:, c, :], scalar=nst[:, c, RSTD:RSTD + 1],
                                                                         in1=C("fin"), op0=ALU.mult, op1=ALU.mult),
                     reads=[hb[c], nsb[c], b_cst], writes=[obb])
                out_dmas.append(S.dma("sp", lambda e, c=c, ob=ob: e.dma_start(out=y_dst[c * 128:(c + 1) * 128, :], in_=ob[:]),
                                      reads=[obb]))

        def emit_state_outputs(nseq, o_conv, o_xbc, o_ssd):
            flat = stg
            stg3 = stg[:, 0:1024].rearrange("p (a b) -> p a b", b=128)
            for q in range(nseq):
                oc = o_conv[q] if nseq > 1 else o_conv
                ox = o_xbc[q] if nseq > 1 else o_xbc
                osd = o_ssd[q] if nseq > 1 else o_ssd
                for ch in range(8):
                    pt = bank(ch % 2)
                    S.op("pe", lambda e, ch=ch, pt=pt, q=q: e.transpose(out=pt[0:32, 0:128], in_=a_last[:, q, ch, :],
                                                                        identity=ident_f[:]),
                         reads=[b_alast, b_idf], writes=[bb[ch % 2]])
                    S.op("act", lambda e, ch=ch, pt=pt: e.activation(out=flat[0:32, ch * 128:(ch + 1) * 128], in_=pt[0:32, 0:128],
                                                                     func=AF.Copy),
                         reads=[bb[ch % 2]], writes=[b_stg, b_ybuf, b_xd])
                out_dmas.append(S.dma("sp", lambda e, oc=oc: e.dma_start(out=oc, in_=flat[2:32, 0:1024]), reads=[b_stg, b_ybuf, b_xd]))
                for xc in range(12):
                    pt = bank(xc % 2)
                    S.op("pe", lambda e, xc=xc, pt=pt, q=q: e.transpose(out=pt[0:4, 0:128], in_=x_last[:, q, xc, :],
                                                                        identity=ident_f[:]),
                         reads=[b_xlast, b_idf], writes=[bb[xc % 2]])
                    S.op("act", lambda e, xc=xc, pt=pt: e.activation(out=flat[0:4, xc * 128:(xc + 1) * 128], in_=pt[0:4, 0:128],
                                                                     func=AF.Copy),
                         reads=[bb[xc % 2]], writes=[b_stg, b_ybuf, b_xd])
                out_dmas.append(S.dma("sp", lambda e, ox=ox: e.dma_start(out=ox, in_=flat[1:4, 0:1536]), reads=[b_stg, b_ybuf, b_xd]))
                for kc in range(8):
                    pt = bank(kc % 2)
                    S.op("pe", lambda e, kc=kc, pt=pt, q=q: e.transpose(out=pt[:, 0:128], in_=St[q][:, kc * 128:(kc + 1) * 128],
                                                                        identity=ident_f[:]),
                         reads=[Stb[q], b_idf], writes=[bb[kc % 2]])
                    S.op("act", lambda e, kc=kc, pt=pt: e.activation(out=stg3[:, kc, :], in_=pt[:, 0:128], func=AF.Copy),
                         reads=[bb[kc % 2]], writes=[b_stg, b_ybuf, b_xd])
                out_dmas.append(S.dma("sp", lambda e, osd=osd: e.dma_start(out=osd.rearrange("(kc p) n -> p kc n", p=128),
                                                                           in_=stg3), reads=[b_stg, b_ybuf, b_xd]))

        def load_x(nch, src):
            for c in range(nch):
                S.dma("sp", lambda e, c=c: e.dma_start(out=h[:, c, :], in_=src[c * 128:(c + 1) * 128, :]), writes=[hb[c]])

        if NT > 0:
            S.op("dve", lambda e: e.memset(St[0][:], 0.0), writes=[Stb[0]])
            S.op("dve", lambda e: e.memset(Sb[0][:], 0.0), writes=[Sbb[0]])
        defer = [None]
        for t in range(NT):
            mode = "full" if t >= NPRE else ("prefix_last" if t == NPRE - 1 else "prefix")
            fcol = flg[:, t:t + 1] if t < NPRE else None
            if t == 0:
                load_x(4, xin[0:TT, :])
                norms(4, "g_ffn1", "B")
            hk = defer[0]
            ffn(4, W["ffn1_w_gate"], W["ffn1_w_up"], W["ffn1_w_down"], "g_ffn1", do_norm=False, hooks=hk, sel="B")
            defer[0] = None
            nxt = (lambda tn=t + 1: load_x(4, xin[tn * TT:(tn + 1) * TT, :])) if t + 1 < NT else None
            nrm = (lambda part=0: norms(4, "g_ffn1", "B", part)) if t + 1 < NT else None
            if mode == "full":
                if t == NPRE:
                    b_ybuf, b_xd = RYX.phase(["ybuf", "xd"])
                mixer(4, [(0, TT)], mode, fcol, t == 0, 1, False, need_last=(t == NT - 1))
                ffn(4, W["ffn2_w_gate"], W["ffn2_w_up"], W["ffn2_w_down"], "g_ffn2")
                tf = t - NPRE
                ple_final(4, pin[tf * TT:(tf + 1) * TT, :], yp[tf * TT:(tf + 1) * TT, :])
                if nxt is not None:
                    nxt()
                    nrm()
            else:
                defer[0] = mixer(4, [(0, TT)], mode, fcol, t == 0, 1, False, hook_after_norm=nxt, hook_after_ssd1=nrm, need_last=False)
        if NFULL > 0:
            emit_state_outputs(1, o_conv_p, o_xbc_p, o_ssd_p)
        if SAMPLE:
            for q in range(2):
                S.dma("sp", lambda e, q=q: e.dma_start(out=stg[:, 0:1024].rearrange("p (a b) -> p a b", b=128),
                                                       in_=sssd[q].rearrange("(kc p) n -> p kc n", p=128)),
                      writes=[b_stg, b_ybuf, b_xd])
                for kc in range(8):
                    pt = bank(kc % 2)
                    S.op("pe", lambda e, kc=kc, pt=pt: e.transpose(out=pt[:, 0:128], in_=stg[:, kc * 128:(kc + 1) * 128], identity=ident_f[:]),
                         reads=[b_stg, b_ybuf, b_xd, b_idf], writes=[bb[kc % 2]])
                    S.op("act", lambda e, kc=kc, pt=pt, q=q: e.activation(out=St[q][:, kc * 128:(kc + 1) * 128], in_=pt[:, 0:128],
                                                                          func=AF.Copy),
                         reads=[bb[kc % 2]], writes=[Stb[q]])
                S.op("act", lambda e, q=q: e.activation(out=Sb[q][:], in_=St[q][:], func=AF.Copy), reads=[Stb[q]], writes=[Sbb[q]])
            load_x(1, xs_in)
            ffn(1, W["ffn1_w_gate"], W["ffn1_w_up"], W["ffn1_w_down"], "g_ffn1")
            mixer(1, [(0, 64), (64, 64)], "full", None, True, 2, True)
            ffn(1, W["ffn2_w_gate"], W["ffn2_w_up"], W["ffn2_w_down"], "g_ffn2")
            ple_final(1, ps_in, ys)
            emit_state_outputs(2, o_conv_s, o_xbc_s, o_ssd_s)
        S.emit(final_waits=out_dmas)
    return nc


NPRE_FULL, NFULL_FULL = 24, 8
_prog_cache = {}


def _get_prog(npre, nfull, sample):
    key = (npre, nfull, sample)
    if key not in _prog_cache:
        _prog_cache[key] = build_program(npre, nfull, sample)
    return _prog_cache[key]


def make_core_inputs(inp, seg_per_seq, npre, nfull):
    B = inp["x_prompt"].shape[0]
    ncores = B * seg_per_seq
    consts = pack_consts(inp)
    masks = make_masks()
    seglen = nfull * TT
    maps = []
    for core in range(ncores):
        b, k = divmod(core, seg_per_seq)
        xin = np.zeros(((npre + nfull) * TT, D), np.float32)
        flags = np.zeros((128, npre + nfull), np.float32)
        npref = k * seglen
        if npref > 0:
            xin[npre * TT - npref:npre * TT] = inp["x_prompt"][b, 0:npref]
            flags[:, npre - npref // TT:npre] = 1.0
        xin[npre * TT:] = inp["x_prompt"][b, k * seglen:(k + 1) * seglen]
        pin = np.ascontiguousarray(inp["p_prompt"][0, b, k * seglen:(k + 1) * seglen])
        m = {"xin": xin, "pin": pin, "flags": flags, "consts": consts, "masks": masks}
        s0 = 2 * core
        m["xs_in"] = np.ascontiguousarray(inp["x_sample"][s0:s0 + 2].reshape(128, D))
        m["ps_in"] = np.ascontiguousarray(inp["p_sample"][0, s0:s0 + 2].reshape(128, 256))
        m["sconv"] = np.ascontiguousarray(inp["state_conv"][0, s0:s0 + 2])
        m["sxbc"] = np.ascontiguousarray(inp["state_ssd_conv"][0, s0:s0 + 2])
        m["sssd"] = np.ascontiguousarray(inp["state_ssd"][0, s0:s0 + 2].reshape(2, 1024, 128))
        for n in WNAMES:
            m[n] = np.ascontiguousarray(inp[n][0])
        maps.append(m)
    return maps


def assemble(res, inp, seg_per_seq, nfull):
    B = inp["x_prompt"].shape[0]
    ncores = B * seg_per_seq
    seglen = nfull * TT
    yp = np.zeros((B, seg_per_seq * seglen, D), np.float32)
    ys = np.zeros((2 * ncores, 64, D), np.float32)
    conv_p = np.zeros((1, B, 30, 1024), np.float32)
    xbc_p = np.zeros((1, B, 3, 1536), np.float32)
    ssd_p = np.zeros((1, B, 16, 64, 128), np.float32)
    conv_s = np.zeros((1, 2 * ncores, 30, 1024), np.float32)
    xbc_s = np.zeros((1, 2 * ncores, 3, 1536), np.float32)
    ssd_s = np.zeros((1, 2 * ncores, 16, 64, 128), np.float32)
    for core in range(ncores):
        r = res[core]
        b, k = divmod(core, seg_per_seq)
        yp[b, k * seglen:(k + 1) * seglen] = r["yp"]
        ys[2 * core:2 * core + 2] = r["ys"].reshape(2, 64, D)
        if k == seg_per_seq - 1:
            conv_p[0, b] = r["o_conv_p"]
            xbc_p[0, b] = r["o_xbc_p"]
            ssd_p[0, b] = r["o_ssd_p"].reshape(16, 64, 128)
        conv_s[0, 2 * core:2 * core + 2] = r["o_conv_s"]
        xbc_s[0, 2 * core:2 * core + 2] = r["o_xbc_s"]
        ssd_s[0, 2 * core:2 * core + 2] = r["o_ssd_s"].reshape(2, 16, 64, 128)
    return yp, ys, conv_p, xbc_p, ssd_p, conv_s, xbc_s, ssd_s


def kernel(**inputs):
    inp = {k: np.asarray(v) for k, v in inputs.items()}
    seg = 4
    nfull = inp["x_prompt"].shape[1] // (seg * TT)
    npre = (seg - 1) * nfull
    nc = _get_prog(npre, nfull, True)
    maps = make_core_inputs(inp, seg, npre, nfull)
    res = run_bass_kernel_spmd(nc, maps, core_ids=list(range(len(maps))))
    return assemble(res.results, inp, seg, nfull)
```
